# Optimizing a Trainium2 kernel written in Bass

```python
import math
import jax, jax.numpy as jnp
from jax import lax
import numpy as np

D_MODEL = 1024
BATCH = 32
SEQ = 256
DEPTH = 4
DEC_BATCH = 4
DEC_SEQ = 4096
PAST_LEN = 256

GRID_W = 64
A_HEADS = 4
A_DK = 128
A_DV = 128
A_KW = A_HEADS * A_DK
A_WIDTH = A_HEADS * A_DV
B_HEADS = 16
B_HEADDIM = 64
B_WIDTH = B_HEADS * B_HEADDIM
B_GROUPS = 4
B_STATE = 64
B_CONV_CH = B_WIDTH + 2 * B_GROUPS * B_STATE
CONV_W = 3
D_FF = 2816
A_CHUNK = 32
B_CHUNK = 64
N_MOD = 6
EPS = 1e-6
IN_SIZES = (A_KW, A_WIDTH, A_KW, A_KW, A_WIDTH, B_WIDTH, B_CONV_CH, B_HEADS, B_HEADS, 2 * D_MODEL)
IN_COLS = sum(IN_SIZES)

kernel_name = 'hybrid_hgrn2_ssd_diffusion_step'


def _split_points():
    pts, acc = [], 0
    for s in IN_SIZES[:-1]:
        acc += s
        pts.append(acc)
    return pts


def _rmsnorm(x, gain):
    xf = x.astype(jnp.float32)
    y = xf * lax.rsqrt(jnp.mean(xf * xf, axis=-1, keepdims=True) + EPS)
    return (y * gain.astype(jnp.float32)).astype(x.dtype)


def _group_rmsnorm(x, gain, groups):
    shp = x.shape
    xf = x.astype(jnp.float32).reshape(shp[:-1] + (groups, shp[-1] // groups))
    y = xf * lax.rsqrt(jnp.mean(xf * xf, axis=-1, keepdims=True) + EPS)
    return (y.reshape(shp) * gain.astype(jnp.float32)).astype(x.dtype)


def _flip(t):
    return jnp.flip(t, axis=1)


def _dwconv(x, w, b):
    l = x.shape[1]
    pad = CONV_W // 2
    xp = jnp.pad(x, ((0, 0), (pad, pad), (0, 0)))
    out = b
    for k in range(CONV_W):
        out = out + xp[:, k:k + l] * w[k]
    return out


def _seq_conv(x, w, b, on_grid):
    if on_grid:
        bsz, l, ch = x.shape
        rows = l // GRID_W
        return _dwconv(x.reshape(bsz * rows, GRID_W, ch), w, b).reshape(bsz, l, ch)
    return _dwconv(x, w, b)


def _chunks(t, csize):
    b, l, h, f = t.shape
    return t.reshape(b, l // csize, csize, h, f).transpose(1, 0, 3, 2, 4)


def _unchunk(t):
    n, b, h, c, f = t.shape
    return t.transpose(1, 0, 3, 2, 4).reshape(b, n * c, h, f)


def _hgrn2_scan(q, k, v, log_f, s0):
    f32 = jnp.float32
    mask = jnp.tril(jnp.ones((A_CHUNK, A_CHUNK), bool))[:, :, None]

    def step(S, inp):
        qc, kc, vc, gc = inp
        bcum = jnp.cumsum(gc, axis=2)
        diff = bcum[:, :, :, None, :] - bcum[:, :, None, :, :]
        decay = jnp.where(mask, jnp.exp(jnp.where(mask, diff, 0.0)), 0.0)
        scores = jnp.einsum('bhtk,bhsk,bhtsk->bhts', qc, kc, decay)
        o = (jnp.einsum('bhts,bhsv->bhtv', scores, vc)
             + jnp.einsum('bhtk,bhkv->bhtv', qc * jnp.exp(bcum), S))
        blast = bcum[:, :, -1:, :]
        S = (jnp.exp(bcum[:, :, -1, :])[..., None] * S
             + jnp.einsum('bhsk,bhsv->bhkv', kc * jnp.exp(blast - bcum), vc))
        return S, o

    xs = tuple(_chunks(t.astype(f32), A_CHUNK) for t in (q, k, v, log_f))
    s_fin, o = lax.scan(step, s0.astype(f32), xs)
    return _unchunk(o), s_fin


def _ssd_scan(xs, dt, a, bm, cm, s0):
    f32 = jnp.float32
    b, l, h, p = xs.shape
    g, n = bm.shape[2], bm.shape[3]
    j = h // g
    nc = l // B_CHUNK
    xc = xs.astype(f32).reshape(b, nc, B_CHUNK, g, j, p).transpose(1, 0, 3, 4, 2, 5)
    dtc = dt.astype(f32).reshape(b, nc, B_CHUNK, g, j).transpose(1, 0, 3, 4, 2)
    ac = a.astype(f32).reshape(b, nc, B_CHUNK, g, j).transpose(1, 0, 3, 4, 2)
    bc = bm.astype(f32).reshape(b, nc, B_CHUNK, g, n).transpose(1, 0, 3, 2, 4)
    cc = cm.astype(f32).reshape(b, nc, B_CHUNK, g, n).transpose(1, 0, 3, 2, 4)
    mask = jnp.tril(jnp.ones((B_CHUNK, B_CHUNK), bool))

    def step(S, inp):
        xq, dq, aq, bq, cq = inp
        acum = jnp.cumsum(aq, axis=-1)
        seg = acum[..., :, None] - acum[..., None, :]
        lmat = jnp.where(mask, jnp.exp(jnp.where(mask, seg, 0.0)), 0.0)
        cb = jnp.einsum('bgtn,bgsn->bgts', cq, bq)
        y = jnp.einsum('bgjts,bgjsp->bgjtp', cb[:, :, None] * lmat, xq * dq[..., None])
        y = y + jnp.einsum('bgtn,bgjpn->bgjtp', cq, S) * jnp.exp(acum)[..., None]
        alast = acum[..., -1:]
        wgt = jnp.exp(alast - acum) * dq
        S = jnp.exp(alast)[..., None] * S + jnp.einsum('bgjs,bgsn,bgjsp->bgjpn', wgt, bq, xq)
        return S, y

    s_fin, y = lax.scan(step, s0.astype(f32).reshape(b, g, j, p, n), (xc, dtc, ac, bc, cc))
    y = y.transpose(1, 0, 4, 2, 3, 5).reshape(b, l, h, p)
    return y, s_fin.reshape(b, h, p, n)


def _mixer(u, on_grid, s_a, s_b, lp):
    f32 = jnp.float32
    bsz, l, _ = u.shape
    proj = u @ lp['w_in']
    q, i_in, f_fw, f_bw, g_a, z, xbc, dt_fw, dt_bw, gates = jnp.split(proj, _split_points(), axis=-1)

    qh = jax.nn.silu(q).reshape(bsz, l, A_HEADS, A_DK)
    vh = i_in.reshape(bsz, l, A_HEADS, A_DV)
    outs_a, fin_a = [], []
    for d, (fraw, s0) in enumerate(((f_fw, s_a[0]), (f_bw, s_a[1]))):
        lbd = lp['lb'][d]
        fx = fraw.astype(f32)
        log_f = jnp.logaddexp(jnp.log(lbd), jnp.log1p(-lbd) + jax.nn.log_sigmoid(fx))
        kk = (1.0 - lbd) * jax.nn.sigmoid(-fx)
        args = (qh, kk.reshape(bsz, l, A_HEADS, A_DK), vh, log_f.reshape(bsz, l, A_HEADS, A_DK))
        if d == 1:
            args = tuple(_flip(t) for t in args)
        o, sf = _hgrn2_scan(*args, s0)
        if d == 1:
            o = _flip(o)
        outs_a.append(o)
        fin_a.append(sf)
    o_a = (outs_a[0] + outs_a[1]).reshape(bsz, l, A_WIDTH).astype(u.dtype)
    o_a = _group_rmsnorm(o_a, lp['a_norm'], A_HEADS) * jax.nn.silu(g_a)

    xbc = jax.nn.silu(_seq_conv(xbc, lp['conv_w'], lp['conv_b'], on_grid))
    xs, bm, cm = jnp.split(xbc, [B_WIDTH, B_WIDTH + B_GROUPS * B_STATE], axis=-1)
    xs = xs.reshape(bsz, l, B_HEADS, B_HEADDIM)
    bm = bm.reshape(bsz, l, B_GROUPS, B_STATE)
    cm = cm.reshape(bsz, l, B_GROUPS, B_STATE)
    outs_b, fin_b = [], []
    for d, (dtraw, s0) in enumerate(((dt_fw, s_b[0]), (dt_bw, s_b[1]))):
        dt = jax.nn.softplus(dtraw.astype(f32) + lp['dt_bias'][d].astype(f32))
        a = -jnp.exp(lp['a_log'][d].astype(f32)) * dt
        args = (xs, dt, a, bm, cm)
        if d == 1:
            args = tuple(_flip(t) for t in args)
        y, sf = _ssd_scan(*args, s0)
        if d == 1:
            y = _flip(y)
        outs_b.append(y)
        fin_b.append(sf)
    y_b = outs_b[0] + outs_b[1] + lp['d_skip'].astype(f32)[:, None] * xs.astype(f32)
    y_b = y_b.reshape(bsz, l, B_WIDTH).astype(u.dtype) * jax.nn.silu(z)
    y_b = _group_rmsnorm(y_b, lp['b_norm'], B_GROUPS)

    g_A, g_B = jnp.split(jax.nn.sigmoid(gates), 2, axis=-1)
    merged = g_A * (o_a @ lp['w_br_a']) + g_B * (y_b @ lp['w_br_b'])
    return merged @ lp['w_out'], (fin_a[0], fin_a[1], fin_b[0], fin_b[1])


def _conv_ffn(u, on_grid, lp):
    h = _seq_conv(u @ lp['w_ff_up'], lp['ff_conv_w'], lp['ff_conv_b'], on_grid)
    a, b = jnp.split(h, 2, axis=-1)
    return (jax.nn.silu(a) * b) @ lp['w_ff_down']


def _block(x, mod, on_grid, s_a, s_b, lp):
    sh1, sc1, g1, sh2, sc2, g2 = jnp.split(mod[:, None, :], N_MOD, axis=-1)
    u = _rmsnorm(x, lp['ln1']) * (1.0 + sc1) + sh1
    mix, fin = _mixer(u, on_grid, s_a, s_b, lp)
    x = x + g1 * mix
    u = _rmsnorm(x, lp['ln2']) * (1.0 + sc2) + sh2
    x = x + g2 * _conv_ffn(u, on_grid, lp)
    return x, fin


def setup_inputs(seed: int = 0) -> dict:
    key = jax.random.key(seed)
    ks = jax.random.split(key, 32)
    f32 = jnp.float32

    def nrm(k, shape, scale):
        return scale * jax.random.normal(k, shape, f32)

    dt0 = jnp.exp(jax.random.uniform(ks[16], (DEPTH, 2, B_HEADS), f32, math.log(1e-3), math.log(1e-1)))
    return {
        'x_prompt': nrm(ks[0], (BATCH, SEQ, D_MODEL), 1.0),
        'x_sample': nrm(ks[1], (DEC_BATCH, DEC_SEQ, D_MODEL), 1.0),
        'c': nrm(ks[2], (DEC_BATCH, D_MODEL), 1.0),
        'state_hgrn': nrm(ks[3], (DEC_BATCH, DEPTH, 2, A_HEADS, A_DK, A_DV), 0.5),
        'state_ssd': nrm(ks[4], (DEC_BATCH, DEPTH, 2, B_HEADS, B_HEADDIM, B_STATE), 0.5),
        'c_ctx': nrm(ks[5], (D_MODEL,), 1.0),
        'w_ada': nrm(ks[6], (DEPTH, D_MODEL, N_MOD * D_MODEL), 0.3 * D_MODEL ** -0.5),
        'b_ada': nrm(ks[7], (DEPTH, N_MOD * D_MODEL), 0.02),
        'ln1': 1.0 + nrm(ks[8], (DEPTH, D_MODEL), 0.02),
        'ln2': 1.0 + nrm(ks[9], (DEPTH, D_MODEL), 0.02),
        'ln_f': 1.0 + nrm(ks[10], (D_MODEL,), 0.02),
        'w_in': nrm(ks[11], (DEPTH, D_MODEL, IN_COLS), D_MODEL ** -0.5),
        'lb_logits': nrm(ks[12], (DEPTH, 2, A_KW), 1.0),
        'a_norm': 1.0 + nrm(ks[13], (DEPTH, A_WIDTH), 0.02),
        'conv_w': nrm(ks[14], (DEPTH, CONV_W, B_CONV_CH), CONV_W ** -0.5),
        'conv_b': nrm(ks[15], (DEPTH, B_CONV_CH), 0.02),
        'dt_bias': dt0 + jnp.log(-jnp.expm1(-dt0)),
        'a_log': jnp.log(jax.random.uniform(ks[17], (DEPTH, 2, B_HEADS), f32, 1.0, 16.0)),
        'd_skip': 1.0 + nrm(ks[18], (DEPTH, B_HEADS), 0.1),
        'b_norm': 1.0 + nrm(ks[19], (DEPTH, B_WIDTH), 0.02),
        'w_br_a': nrm(ks[20], (DEPTH, A_WIDTH, D_MODEL), A_WIDTH ** -0.5),
        'w_br_b': nrm(ks[21], (DEPTH, B_WIDTH, D_MODEL), B_WIDTH ** -0.5),
        'w_out': nrm(ks[22], (DEPTH, D_MODEL, D_MODEL), D_MODEL ** -0.5),
        'w_ff_up': nrm(ks[23], (DEPTH, D_MODEL, 2 * D_FF), D_MODEL ** -0.5),
        'ff_conv_w': nrm(ks[24], (DEPTH, CONV_W, 2 * D_FF), CONV_W ** -0.5),
        'ff_conv_b': nrm(ks[25], (DEPTH, 2 * D_FF), 0.02),
        'w_ff_down': nrm(ks[26], (DEPTH, D_FF, D_MODEL), D_FF ** -0.5),
    }


def reference(x_prompt, x_sample, c, state_hgrn, state_ssd, c_ctx, w_ada, b_ada, ln1, ln2, ln_f,
              w_in, lb_logits, a_norm, conv_w, conv_b, dt_bias, a_log, d_skip, b_norm,
              w_br_a, w_br_b, w_out, w_ff_up, ff_conv_w, ff_conv_b, w_ff_down):
    f32 = jnp.float32
    lb_all = jnp.cumsum(jax.nn.softmax(lb_logits.astype(f32), axis=0), axis=0)
    lb_all = jnp.maximum(lb_all - lb_all[0:1], 0.0)
    layers = [dict(ln1=ln1[l], ln2=ln2[l], w_in=w_in[l], lb=lb_all[l], a_norm=a_norm[l],
                   conv_w=conv_w[l], conv_b=conv_b[l], dt_bias=dt_bias[l], a_log=a_log[l],
                   d_skip=d_skip[l], b_norm=b_norm[l], w_br_a=w_br_a[l], w_br_b=w_br_b[l],
                   w_out=w_out[l], w_ff_up=w_ff_up[l], ff_conv_w=ff_conv_w[l],
                   ff_conv_b=ff_conv_b[l], w_ff_down=w_ff_down[l]) for l in range(DEPTH)]

    bp = x_prompt.shape[0]
    za = jnp.zeros((bp, A_HEADS, A_DK, A_DV), f32)
    zb = jnp.zeros((bp, B_HEADS, B_HEADDIM, B_STATE), f32)
    h = x_prompt
    st_a, st_b = [], []
    for l in range(DEPTH):
        mod = (jax.nn.silu(c_ctx) @ w_ada[l] + b_ada[l])[None]
        h, fin = _block(h, mod, False, (za, za), (zb, zb), layers[l])
        st_a.append(jnp.stack(fin[:2], axis=1))
        st_b.append(jnp.stack(fin[2:], axis=1))
    y_prompt = _rmsnorm(h, ln_f)
    new_state_hgrn = jnp.stack(st_a, axis=1).astype(x_prompt.dtype)
    new_state_ssd = jnp.stack(st_b, axis=1).astype(x_prompt.dtype)

    h = x_sample
    for l in range(DEPTH):
        mod = jax.nn.silu(c) @ w_ada[l] + b_ada[l]
        sa = state_hgrn[:, l]
        sb = state_ssd[:, l]
        h, _ = _block(h, mod, True, (sa[:, 0], sa[:, 1]), (sb[:, 0], sb[:, 1]), layers[l])
    y_sample = _rmsnorm(h, ln_f)
    return (y_prompt, y_sample, new_state_hgrn, new_state_ssd)
```

```python
import contextlib
import numpy as np
import concourse.bass as bass
import concourse.mybir as mybir
from concourse.bass_utils import run_bass_kernel_spmd

F32 = mybir.dt.float32
BF16 = mybir.dt.bfloat16
AF = mybir.ActivationFunctionType
ALU = mybir.AluOpType

D = 1024
T = 4096
L = 4
NT = 8
NB = 32
NSEG = 16
DFF = 2816
INC = 7200
EPS = 1e-6
BIG = 3.0e38


class Tile:
    __slots__ = ("w", "r")

    def __init__(self):
        self.w = None
        self.r = []


class Op:
    __slots__ = ("eng", "fn", "deps", "is_dma", "cost", "nbytes", "region", "sig", "sigval", "dsem", "dval", "dprev", "waits")

    def __init__(self, eng, fn, is_dma, cost, nbytes, region):
        self.eng = eng
        self.fn = fn
        self.deps = ()
        self.is_dma = is_dma
        self.cost = cost
        self.nbytes = nbytes
        self.region = region
        self.sig = False
        self.sigval = 0
        self.dsem = None
        self.dval = 0
        self.dprev = 0
        self.waits = ()


ENGS = ("pe", "act", "dve", "pool", "sp")
NDSEM = {"sp": 16, "pool": 8, "act": 4}
DMA_ISSUE = {"sp": 60.0, "pool": 1000.0, "act": 100.0}
DMA_BW = 150.0
DMA_LAT = 2000.0
SCHED = [True]


class Prog:
    sec = None
    enabled = None

    def __init__(self, nc):
        self.nc = nc
        self.ops = []
        self.es = contextlib.ExitStack()
        self.h = {"pe": nc.tensor, "act": nc.scalar, "dve": nc.vector, "pool": nc.gpsimd, "sp": nc.sync}
        self.region = 0

    def op(self, eng, fn, reads=(), writes=(), is_dma=False, cost=100.0, nbytes=0):
        if self.sec is not None and self.enabled is not None and self.sec not in self.enabled:
            return None
        idx = len(self.ops)
        o = Op(eng, fn, is_dma, cost, nbytes, self.region)
        deps = set()
        for t in reads:
            if t.w is not None:
                deps.add(t.w)
        for t in writes:
            if t.w is not None:
                deps.add(t.w)
            deps.update(t.r)
        deps.discard(idx)
        for t in reads:
            t.r.append(idx)
        for t in writes:
            t.w = idx
            t.r = []
        o.deps = deps
        self.ops.append(o)
        return idx

    def dma(self, eng, out, in_, reads=(), writes=()):
        nb = 1
        for v in out.shape:
            nb *= v
        nb *= 4 if out.dtype == F32 else 2
        return self.op(eng, lambda h: h.dma_start(out=out, in_=in_), reads, writes, is_dma=True, nbytes=nb)

    def barrier(self):
        self.region += 1

    def schedule(self):
        import heapq
        ops = self.ops
        n = len(ops)
        succ = [[] for _ in range(n)]
        indeg = [0] * n
        for i, o in enumerate(ops):
            for d in o.deps:
                if ops[d].region == o.region:
                    succ[d].append(i)
                    indeg[i] += 1
        blev = [0.0] * n
        for i in range(n - 1, -1, -1):
            o = ops[i]
            c = (DMA_LAT + o.nbytes / DMA_BW) if o.is_dma else o.cost
            m = 0.0
            for sidx in succ[i]:
                if blev[sidx] > m:
                    m = blev[sidx]
            blev[i] = m + c
        order = []
        tnow = 0.0
        i0 = 0
        while i0 < n:
            r = ops[i0].region
            i1 = i0
            while i1 < n and ops[i1].region == r:
                i1 += 1
            t0 = tnow
            eng_free = {e: t0 for e in ENGS}
            avail = {e: [] for e in ENGS}
            future = {e: [] for e in ENGS}
            ready = {}
            for i in range(i0, i1):
                if indeg[i] == 0:
                    heapq.heappush(avail[ops[i].eng], (-blev[i], i))
            dma_free = t0
            remaining = i1 - i0
            while remaining:
                best = None
                for e in ENGS:
                    fu = future[e]
                    av = avail[e]
                    ef = eng_free[e]
                    while fu and fu[0][0] <= ef:
                        j_ = heapq.heappop(fu)[1]
                        heapq.heappush(av, (-blev[j_], j_))
                    if av:
                        key = (ef, av[0][1])
                        src = 0
                    elif fu:
                        key = (fu[0][0], fu[0][1])
                        src = 1
                    else:
                        continue
                    if best is None or key < best[0]:
                        best = (key, e, src)
                (st, i), e, src = best
                if src == 0:
                    heapq.heappop(avail[e])
                else:
                    heapq.heappop(future[e])
                o = ops[i]
                if o.is_dma:
                    eng_free[e] = st + DMA_ISSUE[e]
                    xs = max(eng_free[e], dma_free)
                    dma_free = xs + o.nbytes / DMA_BW
                    fin = dma_free + DMA_LAT
                else:
                    fin = st + o.cost
                    eng_free[e] = fin
                if fin > tnow:
                    tnow = fin
                order.append(i)
                for sidx in succ[i]:
                    indeg[sidx] -= 1
                    rt = ready.get(sidx, t0)
                    if fin > rt:
                        rt = fin
                    ready[sidx] = rt
                    if indeg[sidx] == 0:
                        heapq.heappush(future[ops[sidx].eng], (rt, sidx))
                remaining -= 1
            i0 = i1
        self.est_ns = tnow
        return order

    def emit(self):
        nc = self.nc
        ops = self.ops
        n = len(ops)
        order = self.schedule() if SCHED[0] else list(range(n))
        gpos = [0] * n
        for p, i in enumerate(order):
            gpos[i] = p
        sems = {e: self.es.enter_context(nc.semaphore(f"s_{e}")) for e in ENGS}
        dsems = {e: [self.es.enter_context(nc.semaphore(f"d_{e}{i}")) for i in range(k)] for e, k in NDSEM.items()}
        dcnt = {e: 0 for e in NDSEM}
        for i in order:
            o = ops[i]
            if o.is_dma:
                k = NDSEM[o.eng]
                c = dcnt[o.eng]
                dcnt[o.eng] += 1
                o.dsem = dsems[o.eng][c % k]
                o.dval = 16 * (c // k + 1)
                o.dprev = 16 * (c // k)
        last_eng = {e: None for e in ENGS}
        dma_by_region = {}
        eng_barrier = {e: 0 for e in ENGS}
        cur_region = -1
        snap = {}
        for i in order:
            o = ops[i]
            if o.region != cur_region:
                cur_region = o.region
                snap[cur_region] = dict(last_eng)
            deps = set(o.deps)
            if eng_barrier[o.eng] < o.region:
                for e, le in snap[o.region].items():
                    if le is not None:
                        deps.add(le)
                for rr in range(eng_barrier[o.eng], o.region):
                    deps.update(dma_by_region.get(rr, ()))
                eng_barrier[o.eng] = o.region
            chosen = {}
            for d in deps:
                p = ops[d]
                if p.is_dma:
                    key = ("d", id(p.dsem))
                    if key not in chosen or ops[chosen[key]].dval < p.dval:
                        chosen[key] = d
                else:
                    if p.eng == "pe" and o.eng == "pe" and not o.is_dma:
                        continue
                    key = ("e", p.eng)
                    if key not in chosen or gpos[chosen[key]] < gpos[d]:
                        chosen[key] = d
            o.waits = tuple(chosen.values())
            for d in o.waits:
                if not ops[d].is_dma:
                    ops[d].sig = True
            if o.is_dma:
                dma_by_region.setdefault(o.region, []).append(i)
            else:
                last_eng[o.eng] = i
        cnt = {e: 0 for e in ENGS}
        for i in order:
            o = ops[i]
            if not o.is_dma and o.sig:
                cnt[o.eng] += 1
                o.sigval = cnt[o.eng]
        waited = {e: {} for e in ENGS}
        nwait = 0
        for i in order:
            o = ops[i]
            h = self.h[o.eng]
            w = waited[o.eng]
            need = {}
            if o.is_dma and o.dprev > 0:
                need[id(o.dsem)] = (o.dsem, o.dprev)
            for d in o.waits:
                p = ops[d]
                if p.is_dma:
                    s_, v = p.dsem, p.dval
                else:
                    s_, v = sems[p.eng], p.sigval
                k = id(s_)
                if k not in need or need[k][1] < v:
                    need[k] = (s_, v)
            for k, (s_, v) in need.items():
                if w.get(k, 0) < v:
                    h.wait_ge(s_, v)
                    w[k] = v
                    nwait += 1
            ins = o.fn(h)
            if o.is_dma:
                ins.then_inc(o.dsem, 16)
            elif o.sig:
                ins.then_inc(sems[o.eng], 1)
        h = self.h["sp"]
        for e, k in NDSEM.items():
            for i in range(min(k, dcnt[e])):
                total = 16 * ((dcnt[e] - 1 - i) // k + 1)
                h.wait_ge(dsems[e][i], total)
        return dict(nops=n, nwait=nwait, cnt=dict(cnt), dcnt=dict(dcnt), est_ms=getattr(self, "est_ns", 0) / 1e6)


def _n(ap):
    k = 1
    for v in ap.shape[1:]:
        k *= v
    return k


def ACT(P, out, in_, func, r, w, scale=None, bias=None, accum=None):
    kw = {}
    if scale is not None:
        kw["scale"] = scale
    if bias is not None:
        kw["bias"] = bias
    if accum is not None:
        kw["accum_out"] = accum
    return P.op("act", lambda h: h.activation(out=out, in_=in_, func=func, **kw), r, w, cost=(_n(in_) + 230) / 1.2)


def TT(P, eng, out, in0, in1, op, r, w):
    c = (_n(in0) + 110) / 0.96 if eng == "dve" else (1.6 * _n(in0) + 250) / 1.2
    return P.op(eng, lambda h: h.tensor_tensor(out=out, in0=in0, in1=in1, op=op), r, w, cost=c)


def TS(P, eng, out, in0, s1, s2, op0, op1, r, w):
    c = (0.6 * _n(in0) + 110) / 0.96 if eng == "dve" else (1.2 * _n(in0) + 250) / 1.2
    if op1 is None:
        return P.op(eng, lambda h: h.tensor_scalar(out=out, in0=in0, scalar1=s1, scalar2=None, op0=op0), r, w, cost=c)
    return P.op(eng, lambda h: h.tensor_scalar(out=out, in0=in0, scalar1=s1, scalar2=s2, op0=op0, op1=op1), r, w, cost=c)


def STT(P, out, in0, scalar, in1, op0, op1, r, w):
    return P.op("dve", lambda h: h.scalar_tensor_tensor(out=out, in0=in0, scalar=scalar, in1=in1, op0=op0, op1=op1), r, w,
                cost=(_n(in0) + 110) / 0.96)


def CP(P, eng, out, in_, r, w):
    if eng == "act":
        return P.op("act", lambda h: h.activation(out=out, in_=in_, func=AF.Copy), r, w, cost=(_n(in_) + 230) / 1.2)
    c = (0.6 * _n(in_) + 110) / 0.96 if eng == "dve" else (1.2 * _n(in_) + 250) / 1.2
    return P.op(eng, lambda h: h.tensor_copy(out=out, in_=in_), r, w, cost=c)


def MM(P, out, lhsT, rhs, start, stop, r, w, tp=None):
    c = _n(rhs) / 2.4 + 30
    if rhs.dtype == F32:
        c = 4 * _n(rhs) / 2.4 + 10
    if tp is None:
        return P.op("pe", lambda h: h.matmul(out, lhsT=lhsT, rhs=rhs, start=start, stop=stop), r, w, cost=c)
    return P.op("pe", lambda h: h.matmul(out, lhsT=lhsT, rhs=rhs, start=start, stop=stop, tile_position=tp), r, w, cost=c)


def TR(P, out, in_, ident, r, w):
    return P.op("pe", lambda h: h.transpose(out=out, in_=in_, identity=ident), r, w, cost=110.0)


def MEMSET(P, eng, ap, val, w):
    return P.op(eng, lambda h: h.memset(ap, val), (), w, cost=(_n(ap) + 200) / 1.0)


def SCAN(P, out, d0, d1, init, r, w):
    return P.op("dve", lambda h: h.tensor_tensor_scan(out=out, data0=d0, data1=d1, initial=init, op0=ALU.mult, op1=ALU.add), r, w,
                cost=(2 * _n(d0) + 110) / 0.96)


PVL = 290
PV_BADA, PV_LN1, PV_LN2, PV_CW, PV_CB, PV_FW, PV_FB, PV_DTB, PV_ALOG = 0, 48, 56, 64, 100, 112, 244, 288, 289
PV_LB = L * PVL
PV_LNF = PV_LB + 32
NPV = PV_LNF + 8
NROW = 512 + 1024 + 16
C_ID, C_TRIF, C_TRIB, C_HMF, C_HMB, C_R32, C_NR32, C_NR128, C_MNF, C_MNB = 0, 128, 256, 384, 512, 640, 1152, 1664, 1920, 2048
NCON = 2176
NEGM = -30000.0
S_SELH, S_SEL2 = 0, 2048
NSEL = 3072


def build(NL=L, dbg=(), stop=None):
    nc = bass.Bass("TRN2", target_bir_lowering=False)
    P = Prog(nc)

    def din(name, shape, dt=F32):
        return nc.dram_tensor(name, list(shape), dt, kind="ExternalInput").ap()

    def dout(name, shape, dt=F32):
        return nc.dram_tensor(name, list(shape), dt, kind="ExternalOutput").ap()

    def dscr(name, shape, dt):
        kind = "ExternalOutput" if name in dbg else "Internal"
        return nc.dram_tensor(name, list(shape), dt, kind=kind).ap()

    x_in = din("x", [T, D])
    cv_in = din("cv", [128, 8])
    flags_in = din("flags", [128, 2])
    hinit = din("hinit", [L, 2, 128, 512])
    sinit = din("sinit", [L, 2, 128, 512])
    w_ada = din("w_ada", [L, D, 6 * D])
    w_in = din("w_in", [L, D, INC])
    w_br_a = din("w_br_a", [L, 512, D])
    w_br_b = din("w_br_b", [L, D, D])
    w_out = din("w_out", [L, D, D])
    w_up = din("w_ff_up", [L, D, 2 * DFF])
    w_down = din("w_ff_down", [L, DFF, D])
    pv_in = din("pv", [128, NPV])
    rows_in = din("rows", [L, NROW])
    lnf_in = din("lnf", [1, D])
    con_in = din("consts", [128, NCON])
    sel_in = din("sel", [16, NSEL])

    y_out = dout("y", [T, D])
    hst = dout("hst", [NSEG, L, 2, 128, 512])
    sst = dout("sst", [NSEG, L, 2, 128, 512])

    xT = dscr("xT", [8, 128, T], F32)
    qT = dscr("qT", [512, T], BF16)
    fT = dscr("fT", [2, 512, T], F32)
    vS = dscr("vS", [T, 512], BF16)
    gaS = dscr("gaS", [T, 512], BF16)
    zS = dscr("zS", [T, 1024], BF16)
    xbcT = dscr("xbcT", [1536, T], BF16)
    dtT = dscr("dtT", [32, T], F32)
    aT = dscr("aT", [32, T], F32)
    gtT = dscr("gtT", [2048, T], BF16)
    oyS = [dscr("oyF", [T, 1536], F32), dscr("oyB", [T, 1536], F32)]
    h2T = dscr("h2T", [DFF, T], BF16)
    uTd = dscr("uTd", [8, 128, T], BF16) if "uTd" in dbg else None
    u2Td = dscr("u2Td", [8, 128, T], BF16)
    obTd = dscr("obTd", [1536, T], BF16)

    TD = {}

    def td(key):
        if key not in TD:
            TD[key] = Tile()
        return TD[key]

    with P.es:
        es = P.es

        def sb(name, shape, dt):
            return es.enter_context(nc.sbuf_tensor(uq(name), list(shape), dt))

        con = sb("con", [128, NCON], F32); Tcon = Tile()
        pv = sb("pv", [128, NPV], F32); Tpv = Tile()
        flg = sb("flg", [128, 2], F32); Tflg = Tile()
        identb = sb("identb", [128, 128], BF16); Tidb = Tile()
        onesb = sb("onesb", [128, 128], BF16); Tones = Tile()
        modT = sb("modT", [128, L, 48], F32); Tmod = Tile()
        A1 = sb("A1", [128, L, 8], F32)
        A2 = sb("A2", [128, L, 8], F32)
        fA = sb("fA", [128, L, 8], F32); TfA = Tile()
        fB = sb("fB", [128, L, 8], F32)
        nA = sb("nA", [32, L], F32); TnA = Tile()
        cwp = sb("cwp", [128, L, 2, 12], F32); Tcwp = Tile()
        fwp = sb("fwp", [128, L, 2, 44], F32)

        P.dma("sp", con[:], con_in[:, :], (), [Tcon])
        P.dma("sp", pv[:], pv_in[:, :], (), [Tpv])
        P.dma("sp", flg[:], flags_in[:, :], (), [Tflg])
        CP(P, "dve", identb[:], con[:, C_ID:C_ID + 128], [Tcon], [Tidb])
        MEMSET(P, "pool", onesb[:], 1.0, [Tones])
        identf = con[:, C_ID:C_ID + 128]

        def pvl(l, off, n=1):
            return pv[:, l * PVL + off:l * PVL + off + n]

        with contextlib.ExitStack() as es0:
            def sb0(name, shape, dt):
                return es0.enter_context(nc.sbuf_tensor(uq(name), list(shape), dt))

            def ps0(name, shape, dt):
                return es0.enter_context(nc.psum_tensor(uq(name), list(shape), dt))

            cvt = sb0("cvt", [128, 8], F32); Tcv = Tile()
            scv = sb0("scv", [128, 8], F32); Tscv = Tile()
            wad = [sb0(f"wad{i}", [128, 8, 1024], F32) for i in range(2)]; Twad = [Tile(), Tile()]
            psm = ps0("psm", [128, 512], F32); Tpsm = Tile()
            P.dma("sp", cvt[:], cv_in[:, :], (), [Tcv])
            ACT(P, scv[:], cvt[:], AF.Silu, [Tcv], [Tscv])
            it = 0
            for l in range(NL):
                for cch in range(6):
                    wb, Tw = wad[it % 2], Twad[it % 2]
                    it += 1
                    src = w_ada[l].rearrange("(k p) c -> p k c", p=128)[:, :, cch * 1024:(cch + 1) * 1024]
                    P.dma("sp", wb[:], src, (), [Tw])
                    for jj in range(8):
                        j = cch * 8 + jj
                        for k in range(8):
                            MM(P, psm[:, j:j + 1], wb[:, k, jj * 128:(jj + 1) * 128], scv[:, k:k + 1], k == 0, k == 7, [Tw, Tscv], [Tpsm])
                TT(P, "dve", modT[:, l, :], psm[:, 0:48], pvl(l, PV_BADA, 48), ALU.add, [Tpsm, Tpv], [Tmod])
                STT(P, A1[:, l, :], modT[:, l, 8:16], 1.0, pvl(l, PV_LN1, 8), ALU.add, ALU.mult, [Tmod, Tpv], [Tmod])
                STT(P, A2[:, l, :], modT[:, l, 32:40], 1.0, pvl(l, PV_LN2, 8), ALU.add, ALU.mult, [Tmod, Tpv], [Tmod])
                ACT(P, nA[:, l:l + 1], pv[0:32, l * PVL + PV_ALOG:l * PVL + PV_ALOG + 1], AF.Exp, [Tpv], [TnA])
                TS(P, "dve", nA[:, l:l + 1], nA[:, l:l + 1], -1.0, None, ALU.mult, None, [TnA], [TnA])
                for kk_, k in enumerate((0, 2)):
                    TS(P, "dve", cwp[:, l, kk_, :], pvl(l, PV_CW + k * 12, 12), flg[:, 1:2], None, ALU.mult, None, [Tpv, Tflg], [Tcwp])
                    TS(P, "dve", fwp[:, l, kk_, :], pvl(l, PV_FW + k * 44, 44), flg[:, 1:2], None, ALU.mult, None, [Tpv, Tflg], [Tcwp])
            lg = pv[:, PV_LB:PV_LB + 32].rearrange("p (l c) -> p l c", l=4)
            mx = sb0("mx", [128, 8], F32)
            ex = sb0("ex", [128, 4, 8], F32)
            sm = sb0("sm", [128, 8], F32)
            lbt = sb0("lbt", [128, 4, 8], F32)
            Tlb = Tile()
            TT(P, "dve", mx[:], lg[:, 0, :], lg[:, 1, :], ALU.max, [Tpv], [Tlb])
            TT(P, "dve", mx[:], mx[:], lg[:, 2, :], ALU.max, [Tpv, Tlb], [Tlb])
            TT(P, "dve", mx[:], mx[:], lg[:, 3, :], ALU.max, [Tpv, Tlb], [Tlb])
            for l in range(4):
                TT(P, "dve", ex[:, l, :], lg[:, l, :], mx[:], ALU.subtract, [Tpv, Tlb], [Tlb])
            ACT(P, ex[:], ex[:], AF.Exp, [Tlb], [Tlb])
            TT(P, "dve", sm[:], ex[:, 0, :], ex[:, 1, :], ALU.add, [Tlb], [Tlb])
            TT(P, "dve", sm[:], sm[:], ex[:, 2, :], ALU.add, [Tlb], [Tlb])
            TT(P, "dve", sm[:], sm[:], ex[:, 3, :], ALU.add, [Tlb], [Tlb])
            P.op("dve", lambda h: h.reciprocal(out=sm[:], in_=sm[:]), [Tlb], [Tlb], cost=200.0)
            for l in range(4):
                TT(P, "dve", ex[:, l, :], ex[:, l, :], sm[:], ALU.mult, [Tlb], [Tlb])
            MEMSET(P, "dve", lbt[:, 0, :], 0.0, [Tlb])
            CP(P, "dve", lbt[:, 1, :], ex[:, 1, :], [Tlb], [Tlb])
            TT(P, "dve", lbt[:, 2, :], lbt[:, 1, :], ex[:, 2, :], ALU.add, [Tlb], [Tlb])
            TT(P, "dve", lbt[:, 3, :], lbt[:, 2, :], ex[:, 3, :], ALU.add, [Tlb], [Tlb])
            TS(P, "dve", fA[:], lbt[:], -0.5, 0.5, ALU.mult, ALU.add, [Tlb], [TfA])
            TS(P, "dve", fB[:], lbt[:], 0.5, 0.5, ALU.mult, ALU.add, [Tlb], [TfA])

            xl = [sb0(f"xl{i}", [128, D], F32) for i in range(2)]; Txl = [Tile(), Tile()]
            xo = [sb0(f"xo{i}", [128, 8, 128], F32) for i in range(2)]; Txo = [Tile(), Tile()]
            pst = [ps0(f"pst{i}", [128, 4, 128], F32) for i in range(4)]; Tpst = [Tile() for _ in range(4)]
            for b in range(NB):
                s = b % 2
                P.dma("sp", xl[s][:], x_in[b * 128:(b + 1) * 128, :], (), [Txl[s]])
                for hh in range(2):
                    pp = (2 * b + hh) % 4
                    for k4 in range(4):
                        k = hh * 4 + k4
                        TR(P, pst[pp][:, k4, :], xl[s][:, k * 128:(k + 1) * 128], identf, [Txl[s], Tcon], [Tpst[pp]])
                    CP(P, "act" if hh == 0 else "dve", xo[s][:, hh * 4:(hh + 1) * 4, :], pst[pp][:], [Tpst[pp]], [Txo[s]])
                P.dma("sp", xT.rearrange("k p t -> p k t")[:, :, b * 128:(b + 1) * 128], xo[s][:], [Txo[s]], [td(("xTb", b))])
        P.barrier()

        for l in range(NL):
            if stop == "p0":
                break
            with contextlib.ExitStack() as es1:
                def sb1(name, shape, dt):
                    return es1.enter_context(nc.sbuf_tensor(uq(name), list(shape), dt))

                def ps1(name, shape, dt):
                    return es1.enter_context(nc.psum_tensor(uq(name), list(shape), dt))

                uT = sb1("uT", [128, 8, T], BF16); TuT = [Tile() for _ in range(NT)]
                xt = [sb1(f"xt{i}", [128, 8, 512], F32) for i in range(2)]; Txt = [Tile(), Tile()]
                xsq = sb1("xsq", [128, 8, 512], BF16); Txsq = Tile()
                rstd = sb1("rstd", [128, 512], F32); Trstd = Tile()
                tmpn = [sb1(f"tmpn{i}", [128, 512], F32) for i in range(2)]; Ttmpn = [Tile(), Tile()]
                pss = ps1("pss", [128, 512], F32); Tpss = Tile()
                psg = [ps1(f"psg{i}", [128, 512], F32) for i in range(6)]; Tpsg = [Tile() for _ in range(6)]
                wb = [sb1(f"wb{i}", [128, 8, 512], BF16) for i in range(2)]; Twb = [Tile(), Tile()]
                wdt = sb1("wdt", [128, 8, 32], BF16); Twdt = Tile()
                stgb = [sb1(f"stgb{i}", [128, T], BF16) for i in range(2)]; Tstgb = [[Tile() for _ in range(NT)] for _ in range(2)]
                stgf = [sb1(f"stgf{i}", [128, T], F32) for i in range(2)]; Tstgf = [[Tile() for _ in range(NT)] for _ in range(2)]
                stgt = [sb1(f"stgt{i}", [128, 4, 512], BF16) for i in range(2)]; Tstgt = [[Tile() for _ in range(4)] for _ in range(2)]
                acc = [sb1(f"acc{i}", [128, 512], F32) for i in range(2)]; Tacc = [Tile(), Tile()]
                ctm = [sb1(f"ctm{i}", [128, 512], F32) for i in range(2)]; Tctm = [Tile(), Tile()]

                xTv = xT.rearrange("k p t -> p k t")
                for i in range(NT):
                    s = i % 2
                    P.dma("sp", xt[s][:], xTv[:, :, i * 512:(i + 1) * 512], [td(("xT", i))], [Txt[s]])
                    ACT(P, xsq[:], xt[s][:], AF.Square, [Txt[s]], [Txsq])
                    for k in range(8):
                        MM(P, pss[:], onesb[:], xsq[:, k, :], k == 0, k == 7, [Tones, Txsq], [Tpss])
                    ACT(P, rstd[:], pss[:], AF.Ln, [Tpss], [Trstd], scale=1.0 / D, bias=EPS)
                    ACT(P, rstd[:], rstd[:], AF.Exp, [Trstd], [Trstd], scale=-0.5)
                    for k in range(8):
                        tm, Ttm = tmpn[k % 2], Ttmpn[k % 2]
                        STT(P, tm[:], xt[s][:, k, :], A1[:, l, k:k + 1], rstd[:], ALU.mult, ALU.mult, [Txt[s], Tmod, Trstd], [Ttm])
                        ACT(P, uT[:, k, i * 512:(i + 1) * 512], tm[:], AF.Identity, [Ttm, Tmod], [TuT[i]], bias=modT[:, l, k:k + 1])
                if uTd is not None and l == 0:
                    P.dma("sp", uTd.rearrange("k p t -> p k t"), uT[:], TuT, [td("uTd")])

                wsrc = w_in[l].rearrange("(k p) c -> p k c", p=128)
                groups = [("q", 0, 0), ("f", 1024, 0), ("f", 1536, 1), ("xbc", 3584, 0), ("xbc", 4096, 1), ("xbc", 4608, 2),
                          ("gate", 5152, 0), ("gate", 5664, 1), ("gate", 6176, 2), ("gate", 6688, 3),
                          ("v", 512, 0), ("ga", 2048, 0), ("z", 2560, 0), ("z", 3072, 1), ("dt", 5120, 0)]
                gi = 0
                pg = 0
                sbi = 0
                sfi = 0
                sti = 0
                for kind, c0, sub in groups:
                    if kind == "dt":
                        P.dma("pool", wdt[:], wsrc[:, :, c0:c0 + 32], (), [Twdt])
                        sf0, Tsf0 = stgf[sfi % 2], Tstgf[sfi % 2]; sfi += 1
                        sf1, Tsf1 = stgf[sfi % 2], Tstgf[sfi % 2]; sfi += 1
                        for i in range(NT):
                            ps, Tps = psg[pg % 6], Tpsg[pg % 6]; pg += 1
                            for k in range(8):
                                MM(P, ps[0:32, :], wdt[:, k, :], uT[:, k, i * 512:(i + 1) * 512], k == 0, k == 7, [Twdt, TuT[i]], [Tps])
                            sl = slice(i * 512, (i + 1) * 512)
                            ACT(P, sf0[0:32, sl], ps[0:32, :], AF.Exp, [Tps, Tpv], [Tsf0[i]], bias=pv[0:32, l * PVL + PV_DTB:l * PVL + PV_DTB + 1])
                            ACT(P, sf0[0:32, sl], sf0[0:32, sl], AF.Ln, [Tsf0[i]], [Tsf0[i]], bias=1.0)
                            TS(P, "dve", sf1[0:32, sl], sf0[0:32, sl], nA[:, l:l + 1], None, ALU.mult, None, [Tsf0[i], TnA], [Tsf1[i]])
                        P.dma("sp", dtT[:, :], sf0[0:32, :], Tsf0, [td("dtT")])
                        P.dma("sp", aT[:, :], sf1[0:32, :], Tsf1, [td("aT")])
                        continue
                    w, Tw = wb[gi % 2], Twb[gi % 2]; gi += 1
                    P.dma("pool", w[:], wsrc[:, :, c0:c0 + 512], (), [Tw])
                    if kind in ("v", "ga", "z"):
                        dst = {"v": vS, "ga": gaS, "z": zS}[kind]
                        dcol = sub * 512
                        fn = AF.Copy if kind == "v" else AF.Silu
                        for i in range(NT):
                            st, Tst = stgt[sti % 2], Tstgt[sti % 2]; sti += 1
                            for b4 in range(4):
                                b = i * 4 + b4
                                ps, Tps = psg[pg % 6], Tpsg[pg % 6]; pg += 1
                                for k in range(8):
                                    MM(P, ps[:], uT[:, k, b * 128:(b + 1) * 128], w[:, k, :], k == 0, k == 7, [Tw, TuT[i]], [Tps])
                                ACT(P, st[:, b4, :], ps[:], fn, [Tps], [Tst[b4]])
                            dv = dst[i * 512:(i + 1) * 512, dcol:dcol + 512].rearrange("(b p) c -> p b c", p=128)
                            P.dma("sp", dv, st[:], Tst, [td((kind, i))])
                        continue
                    for t4 in range(4):
                        tix = sub * 4 + t4
                        if kind == "f":
                            sg, Tsg = stgf[sfi % 2], Tstgf[sfi % 2]; sfi += 1
                        else:
                            sg, Tsg = stgb[sbi % 2], Tstgb[sbi % 2]; sbi += 1
                        for i in range(NT):
                            ps, Tps = psg[pg % 6], Tpsg[pg % 6]; pg += 1
                            for k in range(8):
                                MM(P, ps[:], w[:, k, t4 * 128:(t4 + 1) * 128], uT[:, k, i * 512:(i + 1) * 512], k == 0, k == 7, [Tw, TuT[i]], [Tps])
                            sl = slice(i * 512, (i + 1) * 512)
                            if kind == "q":
                                ACT(P, sg[:, sl], ps[:], AF.Silu, [Tps], [Tsg[i]])
                            elif kind == "f":
                                a, Ta = acc[i % 2], Tacc[i % 2]
                                ACT(P, a[:], ps[:], AF.Tanh, [Tps], [Ta], scale=0.5)
                                cc = sub * 4 + t4
                                TS(P, "dve", sg[:, sl], a[:], fA[:, l, cc:cc + 1], fB[:, l, cc:cc + 1], ALU.mult, ALU.add, [Ta, TfA], [Tsg[i]])
                            elif kind == "gate":
                                a, Ta = acc[i % 2], Tacc[i % 2]
                                ACT(P, a[:], ps[:], AF.Tanh, [Tps], [Ta], scale=0.5)
                                TS(P, "pool", sg[:, sl], a[:], 0.5, 0.5, ALU.mult, ALU.add, [Ta], [Tsg[i]])
                            else:
                                a, Ta = acc[i % 2], Tacc[i % 2]
                                conv_tile(P, a, Ta, ps, Tps, pvl(l, PV_CW + 12 + tix), pvl(l, PV_CB + tix),
                                          pvl(l, PV_CW + tix), pvl(l, PV_CW + 24 + tix),
                                          cwp[:, l, 0, tix:tix + 1], cwp[:, l, 1, tix:tix + 1], [Tpv, Tcwp], ctm[i % 2], Tctm[i % 2])
                                ACT(P, sg[:, sl], a[:], AF.Silu, [Ta], [Tsg[i]])
                        if kind == "q":
                            P.dma("sp", qT[tix * 128:(tix + 1) * 128, :], sg[:], Tsg, [td(("qT", tix))])
                        elif kind == "f":
                            P.dma("sp", fT[sub, t4 * 128:(t4 + 1) * 128, :], sg[:], Tsg, [td(("fT", sub, t4))])
                        elif kind == "gate":
                            P.dma("sp", gtT[tix * 128:(tix + 1) * 128, :], sg[:], Tsg, [td(("gtT", tix))])
                        else:
                            P.dma("sp", xbcT[tix * 128:(tix + 1) * 128, :], sg[:], Tsg, [td(("xbcT", tix))])
            P.barrier()
            if stop == "p1":
                break

            phase2(P, nc, l, dict(con=con, Tcon=Tcon, identb=identb, Tidb=Tidb, flg=flg, Tflg=Tflg, pv=pv, Tpv=Tpv,
                                  qT=qT, fT=fT, vS=vS, xbcT=xbcT, dtT=dtT, aT=aT, oyS=oyS, hst=hst, sst=sst,
                                  hinit=hinit, sinit=sinit, rows_in=rows_in, sel_in=sel_in))
            P.barrier()
            if stop == "p2":
                break

            phase3(P, nc, l, NL, dict(con=con, Tcon=Tcon, identb=identb, Tidb=Tidb, onesb=onesb, Tones=Tones, pv=pv, Tpv=Tpv,
                                      modT=modT, Tmod=Tmod, A2=A2, fwp=fwp, Tcwp=Tcwp, oyS=oyS, gaS=gaS, zS=zS, gtT=gtT,
                                      xT=xT, h2T=h2T, rows_in=rows_in, w_br_a=w_br_a, w_br_b=w_br_b, w_out=w_out,
                                      w_up=w_up, w_down=w_down, td=td, u2Td=u2Td, obTd=obTd, stop=stop))
            P.barrier()

        with contextlib.ExitStack() as es4:
            def sb4(name, shape, dt):
                return es4.enter_context(nc.sbuf_tensor(uq(name), list(shape), dt))
            lnfB = sb4("lnfB", [128, D], F32); Tlnf = Tile()
            P.dma("sp", lnfB[:], lnf_in.partition_broadcast(128), (), [Tlnf])
            xb = [sb4(f"xb{i}", [128, 8, 128], F32) for i in range(2)]; Txb = [Tile(), Tile()]
            yo = [sb4(f"yo{i}", [128, D], F32) for i in range(2)]; Tyo = [Tile(), Tile()]
            junk = sb4("junk", [128, D], F32); Tjunk = Tile()
            ssq = [sb4(f"ssq{i}", [128, 1], F32) for i in range(2)]; Tssq = [Tile(), Tile()]
            psy = [es4.enter_context(nc.psum_tensor(uq(f"psy{i}"), [128, D], F32)) for i in range(2)]; Tpsy = [Tile(), Tile()]
            xTv = xT.rearrange("k p t -> p k t")
            for b in range(NB):
                s = b % 2
                P.dma("sp", xb[s][:], xTv[:, :, b * 128:(b + 1) * 128], [td(("xT", b // 4))], [Txb[s]])
                for k in range(8):
                    TR(P, psy[s][:, k * 128:(k + 1) * 128], xb[s][:, k, :], identf, [Txb[s], Tcon], [Tpsy[s]])
                ACT(P, junk[:], psy[s][:], AF.Square, [Tpsy[s]], [Tjunk, Tssq[s]], accum=ssq[s][:])
                ACT(P, ssq[s][:], ssq[s][:], AF.Ln, [Tssq[s]], [Tssq[s]], scale=1.0 / D, bias=EPS)
                ACT(P, ssq[s][:], ssq[s][:], AF.Exp, [Tssq[s]], [Tssq[s]], scale=-0.5)
                STT(P, yo[s][:], psy[s][:], ssq[s][:, 0:1], lnfB[:], ALU.mult, ALU.mult, [Tpsy[s], Tssq[s], Tlnf], [Tyo[s]])
                P.dma("sp", y_out[b * 128:(b + 1) * 128, :], yo[s][:], [Tyo[s]], [td(("y", b))])
        stats = P.emit()
    return nc, stats


def conv_tile(P, a, Ta, ps, Tps, w1, b, w0, w2, w0p, w2p, extra, tmp=None, Ttmp=None):
    ACT(P, a[:], ps[:], AF.Identity, [Tps] + extra, [Ta], scale=w1, bias=b)
    av = a[:].rearrange("p (r c) -> p r c", c=64)
    pv_ = ps[:].rearrange("p (r c) -> p r c", c=64)
    STT(P, av[:, :, 1:64], pv_[:, :, 0:63], w0, av[:, :, 1:64], ALU.mult, ALU.add, [Tps, Ta] + extra, [Ta])
    STT(P, av[:, :, 0:63], pv_[:, :, 1:64], w2, av[:, :, 0:63], ALU.mult, ALU.add, [Tps, Ta] + extra, [Ta])
    a4 = a[:].rearrange("p (h r c) -> p h r c", h=2, r=4)
    p4 = ps[:].rearrange("p (h r c) -> p h r c", h=2, r=4)
    STT(P, a4[:, :, 1:4, 0], p4[:, :, 0:3, 63], w0p, a4[:, :, 1:4, 0], ALU.mult, ALU.add, [Tps, Ta] + extra, [Ta])
    STT(P, a4[:, :, 0:3, 63], p4[:, :, 1:4, 0], w2p, a4[:, :, 0:3, 63], ALU.mult, ALU.add, [Tps, Ta] + extra, [Ta])


_UNIQ = [0]


def uq(name):
    _UNIQ[0] += 1
    return f"{name}_u{_UNIQ[0]}"


def mk(ap, dims, off=0):
    return bass.AP(ap.tensor, ap.offset + off, [ap.ap[0]] + list(dims))


import os


def phase2(P, nc, l, G):
    P.enabled = set(os.environ.get('P2OPT', 'load,hprep,sprep,hgrn_a,hu,ho,ssd_a,ssd_a2,ssd_a3,ssd_a4,ssd_l,ssd_c,sy,ss,out').split(','))
    NSTEP = int(os.environ.get('P2STEPS', NSEG))
    con, Tcon, identb, Tidb, flg, Tflg = G["con"], G["Tcon"], G["identb"], G["Tidb"], G["flg"], G["Tflg"]
    qT, fT, vS, xbcT, dtT, aT, oyS, hst, sst = G["qT"], G["fT"], G["vS"], G["xbcT"], G["dtT"], G["aT"], G["oyS"], G["hst"], G["sst"]
    identf = con[:, C_ID:C_ID + 128]
    with contextlib.ExitStack() as es2:
        def sb(name, shape, dt):
            return es2.enter_context(nc.sbuf_tensor(uq(name), list(shape), dt))

        def ps(name, shape, dt):
            return es2.enter_context(nc.psum_tensor(uq(name), list(shape), dt))

        b0 = ps("p2b0", [128, 512], F32); Tb0 = Tile()
        b1 = ps("p2b1", [128, 512], F32); Tb1 = Tile()
        b2 = ps("p2b2", [128, 1024], F32); Tb2 = [Tile(), Tile()]
        b4 = ps("p2b4", [128, 1024], BF16); Tb4 = Tile()
        b5 = ps("p2b5", [128, 1024], F32); Tb5 = Tile()
        b7 = ps("p2b7", [128, 512], F32); Tb7 = Tile()

        sel = sb("sel", [16, NSEL], F32); Tsel = Tile()
        P.dma("sp", sel[:], G["sel_in"][:, :], (), [Tsel])
        dskB = sb("dskB", [128, 16], F32); Tdsk = Tile()
        P.dma("sp", dskB[:], G["rows_in"][l:l + 1, 1536:1552].partition_broadcast(128), (), [Tdsk])

        q_sb = [sb(f"q_sb{i}", [128, 4, 256], BF16) for i in range(2)]
        f_sb = [sb(f"f_sb{i}", [128, 4, 256], F32) for i in range(2)]
        v_sb = [sb(f"v_sb{i}", [128, 2, 512], BF16) for i in range(2)]
        x_sb = [sb(f"x_sb{i}", [128, 12, 256], BF16) for i in range(2)]
        a_sb = [sb(f"a_sb{i}", [16, 256], F32) for i in range(2)]
        stkA = [sb(f"stkA{i}", [32, 256], F32) for i in range(2)]
        stkB = [sb(f"stkB{i}", [16, 256], F32) for i in range(2)]
        ecum = [sb(f"ecum{i}", [16, 256], F32) for i in range(2)]
        Pp = [sb(f"Pp{i}", [128, 4, 256], F32) for i in range(2)]
        qt = [sb(f"qt{i}", [128, 4, 256], BF16) for i in range(2)]
        kt = [sb(f"kt{i}", [128, 4, 256], BF16) for i in range(2)]
        kh = [sb(f"kh{i}", [128, 4, 256], BF16) for i in range(2)]
        Tq, Tf, Tv, Tx, Ta, TsA, TsAd, TsB, Tec, TPp, Tqt, Tkt, Tkh = ([Tile(), Tile()] for _ in range(13))
        d0 = sb("d0", [128, 4, 256], F32); d1 = sb("d1", [128, 4, 256], F32); Td01 = Tile()
        Pinv = sb("Pinv", [128, 4, 256], F32); TPinv = Tile()
        kk = sb("kk", [128, 4, 256], F32); Tkk = Tile()
        ktmp = sb("ktmp", [128, 4, 256], F32); Tktmp = Tile()
        ATm = sb("ATm", [128, 4, 128], BF16); TATm = Tile()
        khtok = sb("khtok", [128, 4, 128], BF16); Tkhtok = Tile()
        qpad = [sb(f"qpad{i}", [128, 4, 4, 128], BF16) for i in range(2)]; Tqpad = [Tile(), Tile()]
        Sbf = [sb(f"Sbf{i}", [128, 4, 4, 128], BF16) for i in range(2)]; TSbf = [[Tile() for _ in range(4)] for _ in range(2)]
        tmpS = sb("tmpS", [128, 512], F32); TtmpS = Tile()
        oy = [sb(f"oy{i}", [128, 1536], F32) for i in range(2)]; Toy = [[Tile(), Tile()] for _ in range(2)]
        mnb = sb("mnb", [128, 2, 4, 128], BF16); Tmnb = Tile()
        ncum = sb("ncum", [128, 16], F32); Tncum = Tile()
        eal = sb("eal", [128, 8], F32); Teal = Tile()
        Btok = sb("Btok", [128, 256], BF16); TBtok = Tile()
        xstok = sb("xstok", [128, 1024], BF16); Txstok = Tile()
        tokA = sb("tokA", [128, 48], F32); TtokA = Tile()
        wt = sb("wt", [128, 16], F32); dtw = sb("dtw", [128, 16], F32); Twt = Tile()
        xdt = sb("xdt", [128, 1024], BF16); Txdt = Tile()
        xw = sb("xw", [128, 1024], BF16); Txw = Tile()
        Lf = sb("Lf", [128, 16, 128], F32); TLf = [Tile() for _ in range(4)]
        Mt = sb("Mt", [128, 16, 128], BF16); TMt = [Tile() for _ in range(4)]
        Ct = sb("Ct", [128, 2, 4, 128], BF16); TCt = Tile()
        STz = sb("STz", [128, 2, 512], BF16); TSTz = Tile()
        CTz = [sb(f"CTz{i}", [128, 2, 2, 256], BF16) for i in range(2)]; TCTz = [Tile(), Tile()]
        Btz = sb("Btz", [128, 2, 2, 128], BF16); TBtz = Tile()
        tmpST = sb("tmpST", [128, 512], F32); TtmpST = Tile()
        dsx = sb("dsx", [128, 1024], BF16); Tdsx = Tile()
        Sh = [[sb(f"Sh{d}_{i}", [128, 512], F32) for i in range(3)] for d in range(2)]; TSh = [[Tile() for _ in range(3)] for _ in range(2)]
        ST = [[sb(f"ST{d}_{i}", [128, 512], F32) for i in range(3)] for d in range(2)]; TST = [[Tile() for _ in range(3)] for _ in range(2)]
        shi = [0, 0]
        sti = [0, 0]
        for i in range(2):
            MEMSET(P, "pool", qpad[i][:], 0.0, [Tqpad[i]])
            MEMSET(P, "pool", CTz[i][:], 0.0, [TCTz[i]])
        MEMSET(P, "pool", STz[:], 0.0, [TSTz])
        for rep in range(4):
            CP(P, "dve", mnb[:, :, rep, :], con[:, C_MNF:C_MNF + 256].rearrange("p (a t) -> p a t", a=2), [Tcon], [Tmnb])
        MEMSET(P, "pool", Btz[:], 0.0, [TBtz])

        def mask3(c0_, n):
            return mk(con[:, c0_:c0_ + n], [(0, 4), (1, n)])

        nstep = 0
        nblk = 0
        for step in range(NSTEP):
            for d in range(2):
                seg = step if d == 0 else NSEG - 1 - step
                s = nstep % 2
                nstep += 1
                tsl = slice(seg * 256, (seg + 1) * 256)
                e32 = 31 if d == 0 else 0
                e128 = 127 if d == 0 else 0
                P.sec = 'load'
                P.dma("sp", q_sb[s][:], qT.rearrange("(h p) t -> p h t", p=128)[:, :, tsl], (), [Tq[s]])
                P.dma("sp", f_sb[s][:], fT[d].rearrange("(h p) t -> p h t", p=128)[:, :, tsl], (), [Tf[s]])
                P.dma("sp", v_sb[s][:], vS[tsl, :].rearrange("(b p) c -> p b c", p=128), (), [Tv[s]])
                P.dma("sp", x_sb[s][:], xbcT.rearrange("(k p) t -> p k t", p=128)[:, :, tsl], (), [Tx[s]])
                P.dma("sp", a_sb[s][:], aT[d * 16:(d + 1) * 16, tsl], (), [Ta[s]])
                P.dma("sp", stkA[s][16:32, :], dtT[d * 16:(d + 1) * 16, tsl], (), [TsAd[s]])
                if step == 0:
                    P.dma("sp", Sh[d][0][:], G["hinit"][l, d], (), [TSh[d][0]])
                    P.dma("sp", ST[d][0][:], G["sinit"][l, d], (), [TST[d][0]])
                P.sec = 'hprep'
                if d == 0:
                    fsrc = f_sb[s][:]
                    pout = Pp[s][:].rearrange("p h t -> p (h t)")
                else:
                    fsrc = mk(f_sb[s][:], [(-256, 4), (-1, 256)], 1023)
                    pout = mk(Pp[s][:], [(-1, 1024)], 1023)
                TT(P, "pool", d0[:], fsrc, mask3(C_NR32, 256), ALU.mult, [Tf[s], Tcon], [Td01])
                TT(P, "pool", d1[:], fsrc, mask3(C_R32, 256), ALU.mult, [Tf[s], Tcon], [Td01])
                SCAN(P, pout, d0[:].rearrange("p h t -> p (h t)"), d1[:].rearrange("p h t -> p (h t)"), 0.0, [Td01], [TPp[s]])
                P.op("dve", lambda h, o=Pinv[:], i=Pp[s][:]: h.reciprocal(out=o, in_=i), [TPp[s]], [TPinv], cost=6400.0)
                TT(P, "pool", qt[s][:], q_sb[s][:], Pp[s][:], ALU.mult, [Tq[s], TPp[s]], [Tqt[s]])
                TS(P, "pool", kk[:], f_sb[s][:], -1.0, 1.0, ALU.mult, ALU.add, [Tf[s]], [Tkk])
                TT(P, "dve", ktmp[:], kk[:], Pinv[:], ALU.mult, [Tkk, TPinv], [Tktmp])
                CP(P, "act", kt[s][:], ktmp[:], [Tktmp], [Tkt[s]])
                TT(P, "dve", kh[s][:].rearrange("p h (c j) -> p (h c) j", j=32), ktmp[:].rearrange("p h (c j) -> p (h c) j", j=32),
                   mk(Pp[s][:], [(32, 32), (0, 32)], e32), ALU.mult, [Tktmp, TPp[s]], [Tkh[s]])
                P.sec = 'sprep'
                if d == 0:
                    SCAN(P, stkA[s][0:16, :], con[0:16, C_NR128:C_NR128 + 256], a_sb[s][:], 0.0, [Ta[s], Tcon], [TsA[s]])
                else:
                    SCAN(P, mk(stkA[s][0:16, :], [(-1, 256)], 255), con[0:16, C_NR128:C_NR128 + 256], mk(a_sb[s][:], [(-1, 256)], 255), 0.0, [Ta[s], Tcon], [TsA[s]])
                ACT(P, ecum[s][:], stkA[s][0:16, :], AF.Exp, [TsA[s]], [Tec[s]])
                CP(P, "pool", CTz[s][0:64, :, 0, :], x_sb[s][0:64, 10:12, :], [Tx[s]], [TCTz[s]])
                CP(P, "pool", CTz[s][64:128, :, 1, :], x_sb[s][64:128, 10:12, :], [Tx[s]], [TCTz[s]])
                TT(P, "dve", stkB[s][:].rearrange("p (b t) -> p b t", b=2), mk(stkA[s][0:16, :], [(128, 2), (0, 128)], e128),
                   stkA[s][0:16, :].rearrange("p (b t) -> p b t", b=2), ALU.subtract, [TsA[s]], [TsB[s]])

                for bi in ((0, 1) if d == 0 else (1, 0)):
                    blk = seg * 2 + bi
                    c0 = bi * 128
                    o = nblk % 2
                    nblk += 1
                    P.sec = 'hgrn_a'
                    for h in range(4):
                        MM(P, b0[:, h * 128:(h + 1) * 128], kt[s][:, h, c0:c0 + 128], qt[s][:, h, c0:c0 + 128], True, True, [Tkt[s], Tqt[s]], [Tb0])
                    TT(P, "dve", ATm[:], b0[:].rearrange("p (h t) -> p h t", h=4), mask3(C_HMF if d == 0 else C_HMB, 128), ALU.mult, [Tb0, Tcon], [TATm])
                    for h in range(4):
                        TR(P, b4[:, h * 128:(h + 1) * 128], kh[s][:, h, c0:c0 + 128], identb[:], [Tkh[s], Tidb], [Tb4])
                    CP(P, "act", khtok[:].rearrange("p h k -> p (h k)"), b4[:, 0:512], [Tb4], [Tkhtok])
                    CP(P, "pool", mk(qpad[o][:], [(512, 4), (160, 4), (1, 32)]), mk(qt[s][:], [(256, 4), (32, 4), (1, 32)], c0), [Tqt[s]], [Tqpad[o]])
                    P.sec = 'hu'
                    for ci, c in enumerate((0, 1, 2, 3) if d == 0 else (3, 2, 1, 0)):
                        cur = shi[d]
                        nxt = (cur + 1) % 3
                        CP(P, "act", Sbf[o][:, c].rearrange("p h v -> p (h v)"), Sh[d][cur][:], [TSh[d][cur]], [TSbf[o][c]])
                        ub = ci % 2
                        for h in range(4):
                            MM(P, b2[:, ub * 512 + h * 128:ub * 512 + (h + 1) * 128], khtok[32 * c:32 * c + 32, h, :],
                               v_sb[s][32 * c:32 * c + 32, bi, h * 128:(h + 1) * 128], True, True, [Tkhtok, Tv[s]], [Tb2[ub]], tp=(32 * c, 0))
                        TT(P, "dve", tmpS[:].rearrange("p (h v) -> p h v", h=4), Sh[d][cur][:].rearrange("p (h v) -> p h v", h=4),
                           mk(Pp[s][:], [(256, 4), (0, 128)], c0 + c * 32 + e32), ALU.mult, [TSh[d][cur], TPp[s]], [TtmpS])
                        TT(P, "dve", Sh[d][nxt][:], tmpS[:], b2[:, ub * 512:(ub + 1) * 512], ALU.add, [TtmpS, Tb2[ub]], [TSh[d][nxt]])
                        shi[d] = nxt
                    P.sec = 'ho'
                    for h in range(4):
                        MM(P, b1[:, h * 128:(h + 1) * 128], ATm[:, h, :], v_sb[s][:, bi, h * 128:(h + 1) * 128], True, False, [TATm, Tv[s]], [Tb1])
                        for c in range(4):
                            MM(P, b1[:, h * 128:(h + 1) * 128], qpad[o][:, h, c, :], Sbf[o][:, c, h, :], False, c == 3, [Tqpad[o], TSbf[o][c]], [Tb1])
                    CP(P, "act", oy[o][:, 0:512], b1[:], [Tb1], [Toy[o][0]])

                    P.sec = 'ssd_a'
                    for g in range(4):
                        MM(P, b0[:, g * 128:(g + 1) * 128], x_sb[s][:, 8 + g // 2, c0:c0 + 128], CTz[s][:, g // 2, g % 2, c0:c0 + 128],
                           True, True, [Tx[s], TCTz[s]], [Tb0])
                    P.sec = 'ssd_a2'
                    for k in range(8):
                        TR(P, b4[:, k * 128:(k + 1) * 128], x_sb[s][:, k, c0:c0 + 128], identb[:], [Tx[s], Tidb], [Tb4])
                    CP(P, "act", xstok[:], b4[:], [Tb4], [Txstok])
                    for k in range(2):
                        TR(P, b4[:, k * 128:(k + 1) * 128], x_sb[s][:, 8 + k, c0:c0 + 128], identb[:], [Tx[s], Tidb], [Tb4])
                    b4v = b4[:, 0:256].rearrange("p (g c) -> p g c", g=2)
                    CP(P, "dve", Btz[:, 0, :, 0:64], b4v[:, :, 0:64], [Tb4], [TBtz])
                    CP(P, "dve", Btz[:, 1, :, 64:128], b4v[:, :, 64:128], [Tb4], [TBtz])
                    P.sec = 'ssd_a3'
                    TR(P, b1[:, 0:32], stkA[s][0:32, c0:c0 + 128], identf[0:32, 0:32], [TsA[s], TsAd[s], Tcon], [Tb1])
                    TR(P, b1[:, 32:48], stkB[s][0:16, c0:c0 + 128], identf[0:16, 0:16], [TsB[s], Tcon], [Tb1])
                    CP(P, "dve", tokA[:], b1[:, 0:48], [Tb1], [TtokA])
                    P.sec = 'ssd_a4'
                    ACT(P, wt[:], tokA[:, 32:48], AF.Exp, [TtokA], [Twt])
                    TT(P, "dve", dtw[:], tokA[:, 16:32], wt[:], ALU.mult, [TtokA, Twt], [Twt])
                    TS(P, "dve", ncum[:], tokA[:, 0:16], -1.0, None, ALU.mult, None, [TtokA], [Tncum])
                    TT(P, "pool", xdt[:].rearrange("p (h q) -> p h q", h=16), xstok[:].rearrange("p (h q) -> p h q", h=16),
                       mk(tokA[:], [(1, 16), (0, 64)], 16), ALU.mult, [Txstok, TtokA], [Txdt])
                    TT(P, "dve", xw[:].rearrange("p (h q) -> p h q", h=16), xstok[:].rearrange("p (h q) -> p h q", h=16),
                       mk(dtw[:], [(1, 16), (0, 64)]), ALU.mult, [Txstok, Twt], [Txw])
                    P.sec = 'ssd_l'
                    for gg in range(4):
                        MM(P, b1[:], identb[:], mnb[:, d].rearrange("p r t -> p (r t)"), True, False, [Tidb, Tmnb], [Tb1])
                        for hh in range(4):
                            hd = gg * 4 + hh
                            MM(P, b1[:, hh * 128:(hh + 1) * 128], sel[0:16, S_SELH + hd * 128:S_SELH + (hd + 1) * 128], stkA[s][0:16, c0:c0 + 128],
                               False, hh == 3, [Tsel, TsA[s]], [Tb1])
                        for hh in range(4):
                            hd = gg * 4 + hh
                            ACT(P, Lf[:, hd, :], b1[:, hh * 128:(hh + 1) * 128], AF.Exp, [Tb1, Tncum], [TLf[gg]], bias=ncum[:, hd:hd + 1])
                        TT(P, "dve", Mt[:, gg * 4:(gg + 1) * 4, :], Lf[:, gg * 4:(gg + 1) * 4, :], mk(b0[:], [(0, 4), (1, 128)], gg * 128), ALU.mult,
                           [TLf[gg], Tb0], [TMt[gg]])
                    P.sec = 'ssd_c'
                    for q in range(8):
                        MM(P, b2[:, q * 128:(q + 1) * 128], sel[0:16, S_SEL2 + q * 128:S_SEL2 + (q + 1) * 128], ecum[s][0:16, c0:c0 + 128],
                           True, True, [Tsel, Tec[s]], [Tb2[q // 4]])
                    TT(P, "dve", Ct[:], mk(x_sb[s][:], [(256, 2), (0, 4), (1, 128)], 10 * 256 + c0), b2[:].rearrange("p (g h t) -> p g h t", g=2, h=4),
                       ALU.mult, [Tx[s], Tb2[0], Tb2[1]], [TCt])
                    P.sec = 'sy'
                    cur = sti[d]
                    nxt = (cur + 1) % 3
                    CP(P, "act", STz[0:64, 0, :], ST[d][cur][0:64, :], [TST[d][cur]], [TSTz])
                    CP(P, "act", STz[64:128, 1, :], ST[d][cur][64:128, :], [TST[d][cur]], [TSTz])
                    if d == 0:
                        TT(P, "pool", dsx[:].rearrange("p (h q) -> p h q", h=16), xstok[:].rearrange("p (h q) -> p h q", h=16),
                           mk(dskB[:], [(1, 16), (0, 64)]), ALU.mult, [Txstok, Tdsk], [Tdsx])
                    for h in range(16):
                        g = h // 4
                        gq, half, hh = g // 2, g % 2, h % 4
                        q = gq * 4 + hh
                        MM(P, b5[:, h * 64:(h + 1) * 64], Mt[:, h, :], xdt[:, h * 64:(h + 1) * 64], True, False, [TMt[g], Txdt], [Tb5])
                        MM(P, b5[:, h * 64:(h + 1) * 64], Ct[:, gq, hh, :], STz[:, half, q * 64:(q + 1) * 64],
                           False, d == 1, [TCt, TSTz], [Tb5])
                        if d == 0:
                            MM(P, b5[:, h * 64:(h + 1) * 64], identb[:], dsx[:, h * 64:(h + 1) * 64], False, True, [Tidb, Tdsx], [Tb5])
                    CP(P, "act", oy[o][:, 512:1536], b5[:], [Tb5], [Toy[o][1]])
                    P.sec = 'ss'
                    for g in range(4):
                        gq, half = g // 2, g % 2
                        MM(P, b7[:, gq * 256:(gq + 1) * 256], Btz[:, half, gq, :], xw[:, g * 256:(g + 1) * 256],
                           half == 0, half == 1, [TBtz, Txw], [Tb7])
                    CP(P, "dve", eal[:], mk(b2[:], [(128, 8)], e128), [Tb2[0], Tb2[1]], [Teal])
                    TT(P, "pool", tmpST[:].rearrange("p (q n) -> p q n", q=8), ST[d][cur][:].rearrange("p (q n) -> p q n", q=8),
                       mk(eal[:], [(1, 8), (0, 64)]), ALU.mult, [TST[d][cur], Teal], [TtmpST])
                    TT(P, "dve", ST[d][nxt][:], tmpST[:], b7[:], ALU.add, [TtmpST, Tb7], [TST[d][nxt]])
                    sti[d] = nxt
                    P.sec = 'out'
                    P.dma("sp", oyS[d][blk * 128:(blk + 1) * 128, :], oy[o][:], Toy[o], ())
                P.sec = 'out'
                cur = shi[d]
                P.dma("sp", hst[seg, l, d], Sh[d][cur][:], [TSh[d][cur]], ())
                cs = sti[d]
                P.dma("sp", sst[seg, l, d], ST[d][cs][:], [TST[d][cs]], ())
                if step < NSEG - 1:
                    nxt = (cur + 1) % 3
                    TS(P, "dve", Sh[d][nxt][:], Sh[d][cur][:], flg[:, 0:1], None, ALU.mult, None, [TSh[d][cur], Tflg], [TSh[d][nxt]])
                    shi[d] = nxt
                    nxt = (cs + 1) % 3
                    TS(P, "dve", ST[d][nxt][:], ST[d][cs][:], flg[:, 0:1], None, ALU.mult, None, [TST[d][cs], Tflg], [TST[d][nxt]])
                    sti[d] = nxt

        P.sec = None


def phase3(P, nc, l, NL, G):
    con, Tcon, identb, Tidb, onesb, Tones, pv, Tpv = G["con"], G["Tcon"], G["identb"], G["Tidb"], G["onesb"], G["Tones"], G["pv"], G["Tpv"]
    modT, Tmod, A2, fwp, Tcwp = G["modT"], G["Tmod"], G["A2"], G["fwp"], G["Tcwp"]
    oyS, gaS, zS, gtT, xT, h2T = G["oyS"], G["gaS"], G["zS"], G["gtT"], G["xT"], G["h2T"]
    u2Td = G["u2Td"]
    xTv = xT.rearrange("k p t -> p k t")
    u2v = u2Td.rearrange("k p t -> p k t")

    def pvl(off, n=1):
        return pv[:, l * PVL + off:l * PVL + off + n]

    obTd = G["obTd"]
    obv = obTd.rearrange("(k p) t -> p k t", p=128)
    with contextlib.ExitStack() as es3:
        def sb(name, shape, dt):
            return es3.enter_context(nc.sbuf_tensor(uq(name), list(shape), dt))

        def ps(name, shape, dt):
            return es3.enter_context(nc.psum_tensor(uq(name), list(shape), dt))

        psTr = [ps(f"psTr{i}", [128, 2048], BF16) for i in range(2)]; TpsTr = [Tile(), Tile()]
        rowsB = sb("rowsB", [128, NROW], F32); Trows = Tile()
        P.dma("sp", rowsB[:], G["rows_in"][l:l + 1, :].partition_broadcast(128), (), [Trows])
        NBUF = 3
        oyf = [sb(f"oyf{i}", [128, 1536], F32) for i in range(NBUF)]
        oyb = [sb(f"oyb{i}", [128, 1536], F32) for i in range(NBUF)]
        gab = [sb(f"gab{i}", [128, 512], BF16) for i in range(NBUF)]
        zb = [sb(f"zb{i}", [128, 1024], BF16) for i in range(NBUF)]
        Toyf, Toyb, Tgab, Tzb = ([Tile() for _ in range(NBUF)] for _ in range(4))
        ssum = [sb(f"ssum{i}", [128, 1536], F32) for i in range(2)]; Tssum = [Tile(), Tile()]
        yz = [sb(f"yz{i}", [128, 1024], F32) for i in range(2)]; Tyz = [Tile(), Tile()]
        junk = sb("junk3", [128, 256], F32); Tjunk = Tile()
        ssq = [sb(f"ssq3_{i}", [128, 8], F32) for i in range(2)]; Tssq = [Tile(), Tile()]
        t1 = [sb(f"t1_{i}", [128, 1536], F32) for i in range(2)]; Tt1 = [Tile(), Tile()]
        t2 = [sb(f"t2_{i}", [128, 512], F32) for i in range(2)]; Tt2 = [Tile(), Tile()]
        on = [sb(f"on{i}", [128, 1536], BF16) for i in range(2)]; Ton = [Tile(), Tile()]
        obs = [sb(f"obs{i}", [128, 12, 128], BF16) for i in range(2)]; Tobs = [Tile(), Tile()]
        for b in range(NB):
            s = b % NBUF
            u = b % 2
            rsl = slice(b * 128, (b + 1) * 128)
            P.dma("sp", oyf[s][:], oyS[0][rsl, :], (), [Toyf[s]])
            P.dma("sp", oyb[s][:], oyS[1][rsl, :], (), [Toyb[s]])
            P.dma("sp", gab[s][:], gaS[rsl, :], (), [Tgab[s]])
            P.dma("sp", zb[s][:], zS[rsl, :], (), [Tzb[s]])
            TT(P, "dve", ssum[u][:], oyf[s][:], oyb[s][:], ALU.add, [Toyf[s], Toyb[s]], [Tssum[u]])
            for h in range(4):
                ACT(P, junk[:, 0:128], ssum[u][:, h * 128:(h + 1) * 128], AF.Square, [Tssum[u]], [Tjunk, Tssq[u]], accum=ssq[u][:, h:h + 1])
            TT(P, "pool", yz[u][:], ssum[u][:, 512:1536], zb[s][:], ALU.mult, [Tssum[u], Tzb[s]], [Tyz[u]])
            for g in range(4):
                ACT(P, junk[:], yz[u][:, g * 256:(g + 1) * 256], AF.Square, [Tyz[u]], [Tjunk, Tssq[u]], accum=ssq[u][:, 4 + g:5 + g])
            ACT(P, ssq[u][:, 0:4], ssq[u][:, 0:4], AF.Ln, [Tssq[u]], [Tssq[u]], scale=1.0 / 128, bias=EPS)
            ACT(P, ssq[u][:, 4:8], ssq[u][:, 4:8], AF.Ln, [Tssq[u]], [Tssq[u]], scale=1.0 / 256, bias=EPS)
            ACT(P, ssq[u][:], ssq[u][:], AF.Exp, [Tssq[u]], [Tssq[u]], scale=-0.5)
            TT(P, "dve", t1[u][:, 0:512].rearrange("p (h v) -> p h v", h=4), ssum[u][:, 0:512].rearrange("p (h v) -> p h v", h=4),
               mk(ssq[u][:], [(1, 4), (0, 128)]), ALU.mult, [Tssum[u], Tssq[u]], [Tt1[u]])
            TT(P, "pool", t2[u][:], t1[u][:, 0:512], rowsB[:, 0:512], ALU.mult, [Tt1[u], Trows], [Tt2[u]])
            TT(P, "dve", on[u][:, 0:512], t2[u][:], gab[s][:], ALU.mult, [Tt2[u], Tgab[s]], [Ton[u]])
            TT(P, "dve", t1[u][:, 512:1536].rearrange("p (g v) -> p g v", g=4), yz[u][:].rearrange("p (g v) -> p g v", g=4),
               mk(ssq[u][:], [(1, 4), (0, 256)], 4), ALU.mult, [Tyz[u], Tssq[u]], [Tt1[u]])
            TT(P, "pool", on[u][:, 512:1536], t1[u][:, 512:1536], rowsB[:, 512:1536], ALU.mult, [Tt1[u], Trows], [Ton[u]])
            for k in range(12):
                TR(P, psTr[u][:, k * 128:(k + 1) * 128], on[u][:, k * 128:(k + 1) * 128], identb[:], [Ton[u], Tidb], [TpsTr[u]])
            CP(P, "act", obs[u][:], psTr[u][:, 0:1536].rearrange("p (k t) -> p k t", k=12), [TpsTr[u]], [Tobs[u]])
            P.dma("sp", obv[:, :, b * 128:(b + 1) * 128], obs[u][:], [Tobs[u]], ())
    P.barrier()

    with contextlib.ExitStack() as es3:
        def sb(name, shape, dt):
            return es3.enter_context(nc.sbuf_tensor(uq(name), list(shape), dt))

        def ps(name, shape, dt):
            return es3.enter_context(nc.psum_tensor(uq(name), list(shape), dt))

        psS = ps("p3s", [128, 512], F32); TpsS = Tile()
        psg = [ps(f"p3g{i}", [128, 512], F32) for i in range(6)]; Tpsg = [Tile() for _ in range(6)]
        wA = sb("wA", [128, 4, D], BF16); wB = sb("wB", [128, 8, D], BF16); wO = sb("wO", [128, 8, D], BF16)
        TwA, TwB, TwO = Tile(), Tile(), Tile()
        P.dma("pool", wA[:], G["w_br_a"][l].rearrange("(k p) c -> p k c", p=128), (), [TwA])
        P.dma("pool", wB[:], G["w_br_b"][l].rearrange("(k p) c -> p k c", p=128), (), [TwB])
        P.dma("pool", wO[:], G["w_out"][l].rearrange("(k p) c -> p k c", p=128), (), [TwO])
        obT = [sb(f"obT{i}", [128, 12, 512], BF16) for i in range(2)]; TobT = [Tile(), Tile()]
        gt = [sb(f"gt{i}", [128, 16, 512], BF16) for i in range(2)]; Tgt = [Tile(), Tile()]
        xt = [sb(f"xt3_{i}", [128, 8, 512], F32) for i in range(2)]; Txt = [Tile(), Tile()]
        m1 = [sb(f"m1_{i}", [128, 512], F32) for i in range(2)]; m2 = [sb(f"m2_{i}", [128, 512], F32) for i in range(2)]
        Tm1, Tm2 = [Tile(), Tile()], [Tile(), Tile()]
        mg = [sb(f"mg{i}", [128, 8, 512], BF16) for i in range(2)]; Tmg = [[Tile() for _ in range(8)] for _ in range(2)]
        xsq = sb("xsq3", [128, 8, 512], BF16); Txsq = Tile()
        rstd = sb("rstd3", [128, 512], F32); Trstd = Tile()
        tmpn = [sb(f"tmpn3_{i}", [128, 512], F32) for i in range(2)]; Ttmpn = [Tile(), Tile()]
        u2s = [sb(f"u2s{i}", [128, 8, 512], BF16) for i in range(2)]; Tu2s = [Tile(), Tile()]
        pg = 0
        for i in range(NT):
            s = i % 2
            tsl = slice(i * 512, (i + 1) * 512)
            P.dma("sp", obT[s][:], obv[:, :, tsl], (), [TobT[s]])
            P.dma("sp", gt[s][:], gtT.rearrange("(k p) t -> p k t", p=128)[:, :, tsl], (), [Tgt[s]])
            P.dma("sp", xt[s][:], xTv[:, :, tsl], (), [Txt[s]])
            for c in range(8):
                pa, Tpa = psg[pg % 6], Tpsg[pg % 6]; pg += 1
                pb, Tpb = psg[pg % 6], Tpsg[pg % 6]; pg += 1
                for k in range(4):
                    MM(P, pa[:], wA[:, k, c * 128:(c + 1) * 128], obT[s][:, k, :], k == 0, k == 3, [TwA, TobT[s]], [Tpa])
                for k in range(8):
                    MM(P, pb[:], wB[:, k, c * 128:(c + 1) * 128], obT[s][:, 4 + k, :], k == 0, k == 7, [TwB, TobT[s]], [Tpb])
                TT(P, "dve", m1[c % 2][:], pa[:], gt[s][:, c, :], ALU.mult, [Tpa, Tgt[s]], [Tm1[c % 2]])
                TT(P, "dve", m2[c % 2][:], pb[:], gt[s][:, 8 + c, :], ALU.mult, [Tpb, Tgt[s]], [Tm2[c % 2]])
                TT(P, "pool", mg[s][:, c, :], m1[c % 2][:], m2[c % 2][:], ALU.add, [Tm1[c % 2], Tm2[c % 2]], [Tmg[s][c]])
            for c in range(8):
                po, Tpo = psg[pg % 6], Tpsg[pg % 6]; pg += 1
                for k in range(8):
                    MM(P, po[:], wO[:, k, c * 128:(c + 1) * 128], mg[s][:, k, :], k == 0, k == 7, [TwO, Tmg[s][k]], [Tpo])
                STT(P, xt[s][:, c, :], po[:], modT[:, l, 16 + c:17 + c], xt[s][:, c, :], ALU.mult, ALU.add, [Tpo, Tmod, Txt[s]], [Txt[s]])
            P.dma("sp", xTv[:, :, tsl], xt[s][:], [Txt[s]], ())
            ACT(P, xsq[:], xt[s][:], AF.Square, [Txt[s]], [Txsq])
            for k in range(8):
                MM(P, psS[:], onesb[:], xsq[:, k, :], k == 0, k == 7, [Tones, Txsq], [TpsS])
            ACT(P, rstd[:], psS[:], AF.Ln, [TpsS], [Trstd], scale=1.0 / D, bias=EPS)
            ACT(P, rstd[:], rstd[:], AF.Exp, [Trstd], [Trstd], scale=-0.5)
            for k in range(8):
                tm, Ttm = tmpn[k % 2], Ttmpn[k % 2]
                STT(P, tm[:], xt[s][:, k, :], A2[:, l, k:k + 1], rstd[:], ALU.mult, ALU.mult, [Txt[s], Tmod, Trstd], [Ttm])
                ACT(P, u2s[s][:, k, :], tm[:], AF.Identity, [Ttm, Tmod], [Tu2s[s]], bias=modT[:, l, 24 + k:25 + k])
            P.dma("sp", u2v[:, :, tsl], u2s[s][:], [Tu2s[s]], ())
    P.barrier()
    if G["stop"] == "p3ab":
        return

    with contextlib.ExitStack() as es3:
        def sb(name, shape, dt):
            return es3.enter_context(nc.sbuf_tensor(uq(name), list(shape), dt))

        def ps(name, shape, dt):
            return es3.enter_context(nc.psum_tensor(uq(name), list(shape), dt))

        psg = [ps(f"p3c{i}", [128, 512], F32) for i in range(6)]; Tpsg = [Tile() for _ in range(6)]
        u2T = sb("u2T", [128, 8, T], BF16); Tu2 = [Tile() for _ in range(NT)]
        for i in range(NT):
            P.dma("sp", u2T[:, :, i * 512:(i + 1) * 512], u2v[:, :, i * 512:(i + 1) * 512], (), [Tu2[i]])
        wu = [sb(f"wu{i}", [128, 8, 2, 128], BF16) for i in range(2)]; Twu = [Tile(), Tile()]
        acA = [sb(f"acA{i}", [128, 512], F32) for i in range(2)]; TacA = [Tile(), Tile()]
        acB = [sb(f"acB{i}", [128, 512], F32) for i in range(2)]; TacB = [Tile(), Tile()]
        sa = [sb(f"sa{i}", [128, 512], F32) for i in range(2)]; Tsa = [Tile(), Tile()]
        ctA = [sb(f"ctA{i}", [128, 512], F32) for i in range(2)]; TctA = [Tile(), Tile()]
        ctB = [sb(f"ctB{i}", [128, 512], F32) for i in range(2)]; TctB = [Tile(), Tile()]
        stg = [sb(f"stg3_{i}", [128, T], BF16) for i in range(2)]; Tstg = [[Tile() for _ in range(NT)] for _ in range(2)]
        wsrc = G["w_up"][l].rearrange("(k p) c -> p k c", p=128)
        pg = 0
        it = 0
        for j in range(22):
            w, Tw = wu[j % 2], Twu[j % 2]
            P.dma("pool", w[:, :, 0, :], wsrc[:, :, j * 128:(j + 1) * 128], (), [Tw])
            P.dma("pool", w[:, :, 1, :], wsrc[:, :, DFF + j * 128:DFF + (j + 1) * 128], (), [Tw])
            sg, Tsg = stg[j % 2], Tstg[j % 2]
            for i in range(NT):
                pa, Tpa = psg[pg % 6], Tpsg[pg % 6]; pg += 1
                pb, Tpb = psg[pg % 6], Tpsg[pg % 6]; pg += 1
                for k in range(8):
                    MM(P, pa[:], w[:, k, 0, :], u2T[:, k, i * 512:(i + 1) * 512], k == 0, k == 7, [Tw, Tu2[i]], [Tpa])
                for k in range(8):
                    MM(P, pb[:], w[:, k, 1, :], u2T[:, k, i * 512:(i + 1) * 512], k == 0, k == 7, [Tw, Tu2[i]], [Tpb])
                s = it % 2
                it += 1
                for (a, Ta, pp, Tpp, tix, ct_, Tct_) in ((acA[s], TacA[s], pa, Tpa, j, ctA[s], TctA[s]), (acB[s], TacB[s], pb, Tpb, 22 + j, ctB[s], TctB[s])):
                    conv_tile(P, a, Ta, pp, Tpp, pvl(PV_FW + 44 + tix), pvl(PV_FB + tix), pvl(PV_FW + tix), pvl(PV_FW + 88 + tix),
                              fwp[:, l, 0, tix:tix + 1], fwp[:, l, 1, tix:tix + 1], [Tpv, Tcwp], ct_, Tct_)
                ACT(P, sa[s][:], acA[s][:], AF.Silu, [TacA[s]], [Tsa[s]])
                TT(P, "pool", sg[:, i * 512:(i + 1) * 512], sa[s][:], acB[s][:], ALU.mult, [Tsa[s], TacB[s]], [Tsg[i]])
            P.dma("sp", h2T[j * 128:(j + 1) * 128, :], sg[:], Tsg, ())
    P.barrier()
    if G["stop"] == "p3c":
        return

    with contextlib.ExitStack() as es3:
        def sb(name, shape, dt):
            return es3.enter_context(nc.sbuf_tensor(uq(name), list(shape), dt))

        def ps(name, shape, dt):
            return es3.enter_context(nc.psum_tensor(uq(name), list(shape), dt))

        psg = [ps(f"p3d{i}", [128, 512], F32) for i in range(6)]; Tpsg = [Tile() for _ in range(6)]
        wd = sb("wd", [128, 22, D], BF16); Twd = [Tile(), Tile()]
        wdsrc = G["w_down"][l].rearrange("(k p) c -> p k c", p=128)
        P.dma("pool", wd[:, 0:11, :], wdsrc[:, 0:11, :], (), [Twd[0]])
        P.dma("pool", wd[:, 11:22, :], wdsrc[:, 11:22, :], (), [Twd[1]])
        h2 = [sb(f"h2_{i}", [128, 22, 512], BF16) for i in range(2)]; Th2 = [Tile(), Tile()]
        xt = [sb(f"xt4_{i}", [128, 8, 512], F32) for i in range(2)]; Txt = [Tile(), Tile()]
        pg = 0
        for i in range(NT):
            s = i % 2
            P.dma("sp", h2[s][:], h2T.rearrange("(k p) t -> p k t", p=128)[:, :, i * 512:(i + 1) * 512], (), [Th2[s]])
            P.dma("sp", xt[s][:], xTv[:, :, i * 512:(i + 1) * 512], (), [Txt[s]])
            for c in range(8):
                po, Tpo = psg[pg % 6], Tpsg[pg % 6]; pg += 1
                for k in range(22):
                    MM(P, po[:], wd[:, k, c * 128:(c + 1) * 128], h2[s][:, k, :], k == 0, k == 21, [Twd[k // 11], Th2[s]], [Tpo])
                STT(P, xt[s][:, c, :], po[:], modT[:, l, 40 + c:41 + c], xt[s][:, c, :], ALU.mult, ALU.add, [Tpo, Tmod, Txt[s]], [Txt[s]])
            P.dma("sp", xTv[:, :, i * 512:(i + 1) * 512], xt[s][:], [Txt[s]], ())


_CACHE = {}


def _consts():
    c = np.zeros((128, NCON), np.float32)
    s = np.arange(128)[:, None]
    t = np.arange(128)[None, :]
    c[:, C_ID:C_ID + 128] = (s == t)
    c[:, C_TRIF:C_TRIF + 128] = (s <= t)
    c[:, C_TRIB:C_TRIB + 128] = (s >= t)
    same = (s // 32) == (t // 32)
    c[:, C_HMF:C_HMF + 128] = same & (s <= t)
    c[:, C_HMB:C_HMB + 128] = same & (s >= t)
    tt = np.arange(512)
    c[:, C_R32:C_R32 + 512] = (tt % 32 == 0)[None, :]
    c[:, C_NR32:C_NR32 + 512] = (tt % 32 != 0)[None, :]
    c[:, C_NR128:C_NR128 + 256] = (np.arange(256) % 128 != 0)[None, :]
    c[:, C_MNF:C_MNF + 128] = np.where(s <= t, 0.0, NEGM)
    c[:, C_MNB:C_MNB + 128] = np.where(s >= t, 0.0, NEGM)
    sel = np.zeros((16, NSEL), np.float32)
    for h in range(16):
        sel[h, S_SELH + h * 128:S_SELH + (h + 1) * 128] = 1.0
    for gq in range(2):
        for hh in range(4):
            q = gq * 4 + hh
            for half in range(2):
                k = (2 * gq + half) * 4 + hh
                sel[k, S_SEL2 + q * 128 + half * 64:S_SEL2 + q * 128 + (half + 1) * 64] = 1.0
    return c, sel


def _pv_rows(inp):
    pv = np.zeros((128, NPV), np.float32)
    f = lambda a: np.asarray(a, np.float32)
    for l in range(L):
        b = l * PVL
        pv[:, b + PV_BADA:b + PV_BADA + 48] = f(inp["b_ada"][l]).reshape(48, 128).T
        pv[:, b + PV_LN1:b + PV_LN1 + 8] = f(inp["ln1"][l]).reshape(8, 128).T
        pv[:, b + PV_LN2:b + PV_LN2 + 8] = f(inp["ln2"][l]).reshape(8, 128).T
        for k in range(3):
            pv[:, b + PV_CW + k * 12:b + PV_CW + (k + 1) * 12] = f(inp["conv_w"][l, k]).reshape(12, 128).T
            pv[:, b + PV_FW + k * 44:b + PV_FW + (k + 1) * 44] = f(inp["ff_conv_w"][l, k]).reshape(44, 128).T
        pv[:, b + PV_CB:b + PV_CB + 12] = f(inp["conv_b"][l]).reshape(12, 128).T
        pv[:, b + PV_FB:b + PV_FB + 44] = f(inp["ff_conv_b"][l]).reshape(44, 128).T
        pv[0:32, b + PV_DTB] = f(inp["dt_bias"][l]).reshape(32)
        pv[0:32, b + PV_ALOG] = f(inp["a_log"][l]).reshape(32)
        for d in range(2):
            pv[:, PV_LB + l * 8 + d * 4:PV_LB + l * 8 + d * 4 + 4] = f(inp["lb_logits"][l, d]).reshape(4, 128).T
    pv[:, PV_LNF:PV_LNF + 8] = f(inp["ln_f"]).reshape(8, 128).T
    rows = np.zeros((L, NROW), np.float32)
    rows[:, 0:512] = f(inp["a_norm"])
    rows[:, 512:1536] = f(inp["b_norm"])
    rows[:, 1536:1552] = f(inp["d_skip"])
    return pv, rows


def make_in_maps(inp):
    f = lambda a: np.ascontiguousarray(np.asarray(a, np.float32))
    con, sel = _consts()
    pv, rows = _pv_rows(inp)
    shared = dict(w_ada=f(inp["w_ada"]), w_in=f(inp["w_in"]), w_br_a=f(inp["w_br_a"]), w_br_b=f(inp["w_br_b"]),
                  w_out=f(inp["w_out"]), w_ff_up=f(inp["w_ff_up"]), w_ff_down=f(inp["w_ff_down"]),
                  pv=pv, rows=rows, lnf=f(inp["ln_f"]).reshape(1, D), consts=con, sel=sel)
    xs = f(inp["x_sample"]); xp = f(inp["x_prompt"])
    sh = f(inp["state_hgrn"]); ss = f(inp["state_ssd"])
    maps = []
    for c in range(8):
        m = dict(shared)
        flags = np.zeros((128, 2), np.float32)
        if c < 4:
            m["x"] = xs[c]
            cv = f(inp["c"])[c]
            flags[:, 0] = 1.0
            m["hinit"] = np.ascontiguousarray(sh[c].transpose(0, 1, 3, 2, 4).reshape(L, 2, 128, 512))
            m["sinit"] = np.ascontiguousarray(ss[c].reshape(L, 2, 2, 2, 4, 64, 64).transpose(0, 1, 3, 6, 2, 4, 5).reshape(L, 2, 128, 512))
        else:
            x = np.zeros((T, D), np.float32)
            x[:2048] = xp[(c - 4) * 8:(c - 3) * 8].reshape(2048, D)
            m["x"] = x
            cv = f(inp["c_ctx"])
            flags[:, 1] = 1.0
            m["hinit"] = np.zeros((L, 2, 128, 512), np.float32)
            m["sinit"] = np.zeros((L, 2, 128, 512), np.float32)
        m["cv"] = np.ascontiguousarray(cv.reshape(8, 128).T)
        m["flags"] = flags
        maps.append(m)
    return maps


_NL = [L]


def kernel(**inp):
    if "nc" not in _CACHE:
        _CACHE["nc"], _CACHE["stats"] = build(NL=_NL[0])
    nc = _CACHE["nc"]
    maps = make_in_maps(inp)
    res = run_bass_kernel_spmd(nc, maps, core_ids=list(range(8)))
    R = res.results
    y_sample = np.stack([np.asarray(R[c]["y"], np.float32) for c in range(4)], axis=0)
    y_prompt = np.concatenate([np.asarray(R[c]["y"], np.float32)[:2048].reshape(8, 256, D) for c in range(4, 8)], axis=0)
    hs = np.concatenate([np.asarray(R[c]["hst"], np.float32)[:8] for c in range(4, 8)], axis=0)
    new_h = np.ascontiguousarray(hs.reshape(32, L, 2, 128, 4, 128).transpose(0, 1, 2, 4, 3, 5))
    st = np.concatenate([np.asarray(R[c]["sst"], np.float32)[:8] for c in range(4, 8)], axis=0)
    new_s = np.ascontiguousarray(st.reshape(32, L, 2, 2, 64, 2, 4, 64).transpose(0, 1, 2, 5, 3, 6, 7, 4).reshape(32, L, 2, 16, 64, 64))
    return (y_prompt, y_sample, new_h, new_s)
```

```python
import contextlib
import numpy as np
import concourse.bass as bass
import concourse.mybir as mybir
from concourse.bass_utils import run_bass_kernel_spmd

F32 = mybir.dt.float32
BF16 = mybir.dt.bfloat16
AF = mybir.ActivationFunctionType
ALU = mybir.AluOpType

D = 1024
T = 4096
L = 4
NT = 8
NB = 32
NSEG = 16
DFF = 2816
INC = 7200
EPS = 1e-6
BIG = 3.0e38


class Tile:
    __slots__ = ("w", "r")

    def __init__(self):
        self.w = None
        self.r = []


class Op:
    __slots__ = ("eng", "fn", "deps", "is_dma", "cost", "nbytes", "region", "sig", "sigval", "dsem", "dval", "dprev", "waits")

    def __init__(self, eng, fn, is_dma, cost, nbytes, region):
        self.eng = eng
        self.fn = fn
        self.deps = ()
        self.is_dma = is_dma
        self.cost = cost
        self.nbytes = nbytes
        self.region = region
        self.sig = False
        self.sigval = 0
        self.dsem = None
        self.dval = 0
        self.dprev = 0
        self.waits = ()


ENGS = ("pe", "act", "dve", "pool", "sp")
NDSEM = {"sp": 16, "pool": 8, "act": 4}
DMA_ISSUE = {"sp": 60.0, "pool": 1000.0, "act": 100.0}
DMA_BW = 150.0
DMA_LAT = 2000.0
SCHED = [True]


class Prog:
    sec = None
    enabled = None

    def __init__(self, nc):
        self.nc = nc
        self.ops = []
        self.es = contextlib.ExitStack()
        self.h = {"pe": nc.tensor, "act": nc.scalar, "dve": nc.vector, "pool": nc.gpsimd, "sp": nc.sync}
        self.region = 0

    def op(self, eng, fn, reads=(), writes=(), is_dma=False, cost=100.0, nbytes=0):
        if self.sec is not None and self.enabled is not None and self.sec not in self.enabled:
            return None
        idx = len(self.ops)
        o = Op(eng, fn, is_dma, cost, nbytes, self.region)
        deps = set()
        for t in reads:
            if t.w is not None:
                deps.add(t.w)
        for t in writes:
            if t.w is not None:
                deps.add(t.w)
            deps.update(t.r)
        deps.discard(idx)
        for t in reads:
            t.r.append(idx)
        for t in writes:
            t.w = idx
            t.r = []
        o.deps = deps
        self.ops.append(o)
        return idx

    def dma(self, eng, out, in_, reads=(), writes=()):
        nb = 1
        for v in out.shape:
            nb *= v
        nb *= 4 if out.dtype == F32 else 2
        return self.op(eng, lambda h: h.dma_start(out=out, in_=in_), reads, writes, is_dma=True, nbytes=nb)

    def barrier(self):
        self.region += 1

    def schedule(self):
        import heapq
        ops = self.ops
        n = len(ops)
        succ = [[] for _ in range(n)]
        indeg = [0] * n
        for i, o in enumerate(ops):
            for d in o.deps:
                if ops[d].region == o.region:
                    succ[d].append(i)
                    indeg[i] += 1
        blev = [0.0] * n
        for i in range(n - 1, -1, -1):
            o = ops[i]
            c = (DMA_LAT + o.nbytes / DMA_BW) if o.is_dma else o.cost
            m = 0.0
            for sidx in succ[i]:
                if blev[sidx] > m:
                    m = blev[sidx]
            blev[i] = m + c
        order = []
        tnow = 0.0
        i0 = 0
        while i0 < n:
            r = ops[i0].region
            i1 = i0
            while i1 < n and ops[i1].region == r:
                i1 += 1
            t0 = tnow
            eng_free = {e: t0 for e in ENGS}
            avail = {e: [] for e in ENGS}
            future = {e: [] for e in ENGS}
            ready = {}
            for i in range(i0, i1):
                if indeg[i] == 0:
                    heapq.heappush(avail[ops[i].eng], (-blev[i], i))
            dma_free = t0
            remaining = i1 - i0
            while remaining:
                best = None
                for e in ENGS:
                    fu = future[e]
                    av = avail[e]
                    ef = eng_free[e]
                    while fu and fu[0][0] <= ef:
                        j_ = heapq.heappop(fu)[1]
                        heapq.heappush(av, (-blev[j_], j_))
                    if av:
                        key = (ef, av[0][1])
                        src = 0
                    elif fu:
                        key = (fu[0][0], fu[0][1])
                        src = 1
                    else:
                        continue
                    if best is None or key < best[0]:
                        best = (key, e, src)
                (st, i), e, src = best
                if src == 0:
                    heapq.heappop(avail[e])
                else:
                    heapq.heappop(future[e])
                o = ops[i]
                if o.is_dma:
                    eng_free[e] = st + DMA_ISSUE[e]
                    xs = max(eng_free[e], dma_free)
                    dma_free = xs + o.nbytes / DMA_BW
                    fin = dma_free + DMA_LAT
                else:
                    fin = st + o.cost
                    eng_free[e] = fin
                if fin > tnow:
                    tnow = fin
                order.append(i)
                for sidx in succ[i]:
                    indeg[sidx] -= 1
                    rt = ready.get(sidx, t0)
                    if fin > rt:
                        rt = fin
                    ready[sidx] = rt
                    if indeg[sidx] == 0:
                        heapq.heappush(future[ops[sidx].eng], (rt, sidx))
                remaining -= 1
            i0 = i1
        self.est_ns = tnow
        return order

    def emit(self):
        nc = self.nc
        ops = self.ops
        n = len(ops)
        order = self.schedule() if SCHED[0] else list(range(n))
        gpos = [0] * n
        for p, i in enumerate(order):
            gpos[i] = p
        sems = {e: self.es.enter_context(nc.semaphore(f"s_{e}")) for e in ENGS}
        dsems = {e: [self.es.enter_context(nc.semaphore(f"d_{e}{i}")) for i in range(k)] for e, k in NDSEM.items()}
        dcnt = {e: 0 for e in NDSEM}
        for i in order:
            o = ops[i]
            if o.is_dma:
                k = NDSEM[o.eng]
                c = dcnt[o.eng]
                dcnt[o.eng] += 1
                o.dsem = dsems[o.eng][c % k]
                o.dval = 16 * (c // k + 1)
                o.dprev = 16 * (c // k)
        last_eng = {e: None for e in ENGS}
        dma_by_region = {}
        eng_barrier = {e: 0 for e in ENGS}
        cur_region = -1
        snap = {}
        for i in order:
            o = ops[i]
            if o.region != cur_region:
                cur_region = o.region
                snap[cur_region] = dict(last_eng)
            deps = set(o.deps)
            if eng_barrier[o.eng] < o.region:
                for e, le in snap[o.region].items():
                    if le is not None:
                        deps.add(le)
                for rr in range(eng_barrier[o.eng], o.region):
                    deps.update(dma_by_region.get(rr, ()))
                eng_barrier[o.eng] = o.region
            chosen = {}
            for d in deps:
                p = ops[d]
                if p.is_dma:
                    key = ("d", id(p.dsem))
                    if key not in chosen or ops[chosen[key]].dval < p.dval:
                        chosen[key] = d
                else:
                    if p.eng == "pe" and o.eng == "pe" and not o.is_dma:
                        continue
                    key = ("e", p.eng)
                    if key not in chosen or gpos[chosen[key]] < gpos[d]:
                        chosen[key] = d
            o.waits = tuple(chosen.values())
            for d in o.waits:
                if not ops[d].is_dma:
                    ops[d].sig = True
            if o.is_dma:
                dma_by_region.setdefault(o.region, []).append(i)
            else:
                last_eng[o.eng] = i
        cnt = {e: 0 for e in ENGS}
        for i in order:
            o = ops[i]
            if not o.is_dma and o.sig:
                cnt[o.eng] += 1
                o.sigval = cnt[o.eng]
        waited = {e: {} for e in ENGS}
        nwait = 0
        for i in order:
            o = ops[i]
            h = self.h[o.eng]
            w = waited[o.eng]
            need = {}
            if o.is_dma and o.dprev > 0:
                need[id(o.dsem)] = (o.dsem, o.dprev)
            for d in o.waits:
                p = ops[d]
                if p.is_dma:
                    s_, v = p.dsem, p.dval
                else:
                    s_, v = sems[p.eng], p.sigval
                k = id(s_)
                if k not in need or need[k][1] < v:
                    need[k] = (s_, v)
            for k, (s_, v) in need.items():
                if w.get(k, 0) < v:
                    h.wait_ge(s_, v)
                    w[k] = v
                    nwait += 1
            ins = o.fn(h)
            if o.is_dma:
                ins.then_inc(o.dsem, 16)
            elif o.sig:
                ins.then_inc(sems[o.eng], 1)
        h = self.h["sp"]
        for e, k in NDSEM.items():
            for i in range(min(k, dcnt[e])):
                total = 16 * ((dcnt[e] - 1 - i) // k + 1)
                h.wait_ge(dsems[e][i], total)
        return dict(nops=n, nwait=nwait, cnt=dict(cnt), dcnt=dict(dcnt), est_ms=getattr(self, "est_ns", 0) / 1e6)


def _n(ap):
    k = 1
    for v in ap.shape[1:]:
        k *= v
    return k


def ACT(P, out, in_, func, r, w, scale=None, bias=None, accum=None):
    kw = {}
    if scale is not None:
        kw["scale"] = scale
    if bias is not None:
        kw["bias"] = bias
    if accum is not None:
        kw["accum_out"] = accum
    return P.op("act", lambda h: h.activation(out=out, in_=in_, func=func, **kw), r, w, cost=(_n(in_) + 230) / 1.2)


def TT(P, eng, out, in0, in1, op, r, w):
    c = (_n(in0) + 110) / 0.96 if eng == "dve" else (1.6 * _n(in0) + 250) / 1.2
    return P.op(eng, lambda h: h.tensor_tensor(out=out, in0=in0, in1=in1, op=op), r, w, cost=c)


def TS(P, eng, out, in0, s1, s2, op0, op1, r, w):
    c = (0.6 * _n(in0) + 110) / 0.96 if eng == "dve" else (1.2 * _n(in0) + 250) / 1.2
    if op1 is None:
        return P.op(eng, lambda h: h.tensor_scalar(out=out, in0=in0, scalar1=s1, scalar2=None, op0=op0), r, w, cost=c)
    return P.op(eng, lambda h: h.tensor_scalar(out=out, in0=in0, scalar1=s1, scalar2=s2, op0=op0, op1=op1), r, w, cost=c)


def STT(P, out, in0, scalar, in1, op0, op1, r, w):
    return P.op("dve", lambda h: h.scalar_tensor_tensor(out=out, in0=in0, scalar=scalar, in1=in1, op0=op0, op1=op1), r, w,
                cost=(_n(in0) + 110) / 0.96)


def CP(P, eng, out, in_, r, w):
    if eng == "act":
        return P.op("act", lambda h: h.activation(out=out, in_=in_, func=AF.Copy), r, w, cost=(_n(in_) + 230) / 1.2)
    c = (0.6 * _n(in_) + 110) / 0.96 if eng == "dve" else (1.2 * _n(in_) + 250) / 1.2
    return P.op(eng, lambda h: h.tensor_copy(out=out, in_=in_), r, w, cost=c)


def MM(P, out, lhsT, rhs, start, stop, r, w, tp=None):
    c = max(64, _n(rhs)) / 2.2 + 8
    if rhs.dtype == F32:
        c *= 4
    if tp is None:
        return P.op("pe", lambda h: h.matmul(out, lhsT=lhsT, rhs=rhs, start=start, stop=stop), r, w, cost=c)
    return P.op("pe", lambda h: h.matmul(out, lhsT=lhsT, rhs=rhs, start=start, stop=stop, tile_position=tp), r, w, cost=c)


def TR(P, out, in_, ident, r, w):
    return P.op("pe", lambda h: h.transpose(out=out, in_=in_, identity=ident), r, w, cost=70.0)


def MEMSET(P, eng, ap, val, w):
    return P.op(eng, lambda h: h.memset(ap, val), (), w, cost=(_n(ap) + 200) / 1.0)


def SCAN(P, out, d0, d1, init, r, w):
    return P.op("dve", lambda h: h.tensor_tensor_scan(out=out, data0=d0, data1=d1, initial=init, op0=ALU.mult, op1=ALU.add), r, w,
                cost=(2 * _n(d0) + 110) / 0.96)


PVL = 290
PV_BADA, PV_LN1, PV_LN2, PV_CW, PV_CB, PV_FW, PV_FB, PV_DTB, PV_ALOG = 0, 48, 56, 64, 100, 112, 244, 288, 289
PV_LB = L * PVL
PV_LNF = PV_LB + 32
NPV = PV_LNF + 8
NROW = 512 + 1024 + 16
C_ID, C_TRIF, C_TRIB, C_HMF, C_HMB, C_R32, C_NR32, C_NR128, C_MNF, C_MNB = 0, 128, 256, 384, 512, 640, 1152, 1664, 1920, 2048
NCON = 2176
NEGM = -30000.0
S_SELH, S_SEL2 = 0, 2048
NSEL = 3072


def build(NL=L, dbg=(), stop=None):
    nc = bass.Bass("TRN2", target_bir_lowering=False)
    P = Prog(nc)

    def din(name, shape, dt=F32):
        return nc.dram_tensor(name, list(shape), dt, kind="ExternalInput").ap()

    def dout(name, shape, dt=F32):
        return nc.dram_tensor(name, list(shape), dt, kind="ExternalOutput").ap()

    def dscr(name, shape, dt):
        kind = "ExternalOutput" if name in dbg else "Internal"
        return nc.dram_tensor(name, list(shape), dt, kind=kind).ap()

    x_in = din("x", [T, D])
    cv_in = din("cv", [128, 8])
    flags_in = din("flags", [128, 2])
    hinit = din("hinit", [L, 2, 128, 512])
    sinit = din("sinit", [L, 2, 128, 512])
    w_ada = din("w_ada", [L, D, 6 * D])
    w_in = din("w_in", [L, D, INC])
    w_br_a = din("w_br_a", [L, 512, D])
    w_br_b = din("w_br_b", [L, D, D])
    w_out = din("w_out", [L, D, D])
    w_up = din("w_ff_up", [L, D, 2 * DFF])
    w_down = din("w_ff_down", [L, DFF, D])
    pv_in = din("pv", [128, NPV])
    rows_in = din("rows", [L, NROW])
    lnf_in = din("lnf", [1, D])
    con_in = din("consts", [128, NCON])
    sel_in = din("sel", [16, NSEL])

    y_out = dout("y", [T, D])
    hst = dout("hst", [NSEG, L, 2, 128, 512])
    sst = dout("sst", [NSEG, L, 2, 128, 512])

    xT = dscr("xT", [8, 128, T], F32)
    qT = dscr("qT", [512, T], BF16)
    fT = dscr("fT", [2, 512, T], F32)
    vS = dscr("vS", [T, 512], BF16)
    gaS = dscr("gaS", [T, 512], BF16)
    zS = dscr("zS", [T, 1024], BF16)
    xbcT = dscr("xbcT", [1536, T], BF16)
    dtT = dscr("dtT", [32, T], F32)
    aT = dscr("aT", [32, T], F32)
    gtT = dscr("gtT", [2048, T], BF16)
    oyS = [dscr("oyF", [T, 1536], F32), dscr("oyB", [T, 1536], F32)]
    h2T = dscr("h2T", [DFF, T], BF16)
    uTd = dscr("uTd", [8, 128, T], BF16) if "uTd" in dbg else None
    u2Td = dscr("u2Td", [8, 128, T], BF16)
    obTd = dscr("obTd", [1536, T], BF16)

    TD = {}

    def td(key):
        if key not in TD:
            TD[key] = Tile()
        return TD[key]

    with P.es:
        es = P.es

        def sb(name, shape, dt):
            return es.enter_context(nc.sbuf_tensor(uq(name), list(shape), dt))

        con = sb("con", [128, NCON], F32); Tcon = Tile()
        pv = sb("pv", [128, NPV], F32); Tpv = Tile()
        flg = sb("flg", [128, 2], F32); Tflg = Tile()
        identb = sb("identb", [128, 128], BF16); Tidb = Tile()
        onesb = sb("onesb", [128, 128], BF16); Tones = Tile()
        modT = sb("modT", [128, L, 48], F32); Tmod = Tile()
        A1 = sb("A1", [128, L, 8], F32)
        A2 = sb("A2", [128, L, 8], F32)
        fA = sb("fA", [128, L, 8], F32); TfA = Tile()
        fB = sb("fB", [128, L, 8], F32)
        nA = sb("nA", [32, L], F32); TnA = Tile()
        cwp = sb("cwp", [128, L, 2, 12], F32); Tcwp = Tile()
        fwp = sb("fwp", [128, L, 2, 44], F32)

        P.dma("sp", con[:], con_in[:, :], (), [Tcon])
        P.dma("sp", pv[:], pv_in[:, :], (), [Tpv])
        P.dma("sp", flg[:], flags_in[:, :], (), [Tflg])
        CP(P, "dve", identb[:], con[:, C_ID:C_ID + 128], [Tcon], [Tidb])
        MEMSET(P, "pool", onesb[:], 1.0, [Tones])
        identf = con[:, C_ID:C_ID + 128]

        def pvl(l, off, n=1):
            return pv[:, l * PVL + off:l * PVL + off + n]

        with contextlib.ExitStack() as es0:
            def sb0(name, shape, dt):
                return es0.enter_context(nc.sbuf_tensor(uq(name), list(shape), dt))

            def ps0(name, shape, dt):
                return es0.enter_context(nc.psum_tensor(uq(name), list(shape), dt))

            cvt = sb0("cvt", [128, 8], F32); Tcv = Tile()
            scv = sb0("scv", [128, 8], F32); Tscv = Tile()
            wad = [sb0(f"wad{i}", [128, 8, 1024], F32) for i in range(2)]; Twad = [Tile(), Tile()]
            psm = ps0("psm", [128, 512], F32); Tpsm = Tile()
            P.dma("sp", cvt[:], cv_in[:, :], (), [Tcv])
            ACT(P, scv[:], cvt[:], AF.Silu, [Tcv], [Tscv])
            it = 0
            for l in range(NL):
                for cch in range(6):
                    wb, Tw = wad[it % 2], Twad[it % 2]
                    it += 1
                    src = w_ada[l].rearrange("(k p) c -> p k c", p=128)[:, :, cch * 1024:(cch + 1) * 1024]
                    P.dma("sp", wb[:], src, (), [Tw])
                    for jj in range(8):
                        j = cch * 8 + jj
                        for k in range(8):
                            MM(P, psm[:, j:j + 1], wb[:, k, jj * 128:(jj + 1) * 128], scv[:, k:k + 1], k == 0, k == 7, [Tw, Tscv], [Tpsm])
                TT(P, "dve", modT[:, l, :], psm[:, 0:48], pvl(l, PV_BADA, 48), ALU.add, [Tpsm, Tpv], [Tmod])
                STT(P, A1[:, l, :], modT[:, l, 8:16], 1.0, pvl(l, PV_LN1, 8), ALU.add, ALU.mult, [Tmod, Tpv], [Tmod])
                STT(P, A2[:, l, :], modT[:, l, 32:40], 1.0, pvl(l, PV_LN2, 8), ALU.add, ALU.mult, [Tmod, Tpv], [Tmod])
                ACT(P, nA[:, l:l + 1], pv[0:32, l * PVL + PV_ALOG:l * PVL + PV_ALOG + 1], AF.Exp, [Tpv], [TnA])
                TS(P, "dve", nA[:, l:l + 1], nA[:, l:l + 1], -1.0, None, ALU.mult, None, [TnA], [TnA])
                for kk_, k in enumerate((0, 2)):
                    TS(P, "dve", cwp[:, l, kk_, :], pvl(l, PV_CW + k * 12, 12), flg[:, 1:2], None, ALU.mult, None, [Tpv, Tflg], [Tcwp])
                    TS(P, "dve", fwp[:, l, kk_, :], pvl(l, PV_FW + k * 44, 44), flg[:, 1:2], None, ALU.mult, None, [Tpv, Tflg], [Tcwp])
            lg = pv[:, PV_LB:PV_LB + 32].rearrange("p (l c) -> p l c", l=4)
            mx = sb0("mx", [128, 8], F32)
            ex = sb0("ex", [128, 4, 8], F32)
            sm = sb0("sm", [128, 8], F32)
            lbt = sb0("lbt", [128, 4, 8], F32)
            Tlb = Tile()
            TT(P, "dve", mx[:], lg[:, 0, :], lg[:, 1, :], ALU.max, [Tpv], [Tlb])
            TT(P, "dve", mx[:], mx[:], lg[:, 2, :], ALU.max, [Tpv, Tlb], [Tlb])
            TT(P, "dve", mx[:], mx[:], lg[:, 3, :], ALU.max, [Tpv, Tlb], [Tlb])
            for l in range(4):
                TT(P, "dve", ex[:, l, :], lg[:, l, :], mx[:], ALU.subtract, [Tpv, Tlb], [Tlb])
            ACT(P, ex[:], ex[:], AF.Exp, [Tlb], [Tlb])
            TT(P, "dve", sm[:], ex[:, 0, :], ex[:, 1, :], ALU.add, [Tlb], [Tlb])
            TT(P, "dve", sm[:], sm[:], ex[:, 2, :], ALU.add, [Tlb], [Tlb])
            TT(P, "dve", sm[:], sm[:], ex[:, 3, :], ALU.add, [Tlb], [Tlb])
            P.op("dve", lambda h: h.reciprocal(out=sm[:], in_=sm[:]), [Tlb], [Tlb], cost=200.0)
            for l in range(4):
                TT(P, "dve", ex[:, l, :], ex[:, l, :], sm[:], ALU.mult, [Tlb], [Tlb])
            MEMSET(P, "dve", lbt[:, 0, :], 0.0, [Tlb])
            CP(P, "dve", lbt[:, 1, :], ex[:, 1, :], [Tlb], [Tlb])
            TT(P, "dve", lbt[:, 2, :], lbt[:, 1, :], ex[:, 2, :], ALU.add, [Tlb], [Tlb])
            TT(P, "dve", lbt[:, 3, :], lbt[:, 2, :], ex[:, 3, :], ALU.add, [Tlb], [Tlb])
            TS(P, "dve", fA[:], lbt[:], -0.5, 0.5, ALU.mult, ALU.add, [Tlb], [TfA])
            TS(P, "dve", fB[:], lbt[:], 0.5, 0.5, ALU.mult, ALU.add, [Tlb], [TfA])

            xl = [sb0(f"xl{i}", [128, D], F32) for i in range(2)]; Txl = [Tile(), Tile()]
            xo = [sb0(f"xo{i}", [128, 8, 128], F32) for i in range(2)]; Txo = [Tile(), Tile()]
            pst = [ps0(f"pst{i}", [128, 4, 128], F32) for i in range(4)]; Tpst = [Tile() for _ in range(4)]
            for b in range(NB):
                s = b % 2
                P.dma("sp", xl[s][:], x_in[b * 128:(b + 1) * 128, :], (), [Txl[s]])
                for hh in range(2):
                    pp = (2 * b + hh) % 4
                    for k4 in range(4):
                        k = hh * 4 + k4
                        TR(P, pst[pp][:, k4, :], xl[s][:, k * 128:(k + 1) * 128], identf, [Txl[s], Tcon], [Tpst[pp]])
                    CP(P, "act" if hh == 0 else "dve", xo[s][:, hh * 4:(hh + 1) * 4, :], pst[pp][:], [Tpst[pp]], [Txo[s]])
                P.dma("sp", xT.rearrange("k p t -> p k t")[:, :, b * 128:(b + 1) * 128], xo[s][:], [Txo[s]], [td(("xTb", b))])
        P.barrier()

        for l in range(NL):
            if stop == "p0":
                break
            with contextlib.ExitStack() as es1:
                def sb1(name, shape, dt):
                    return es1.enter_context(nc.sbuf_tensor(uq(name), list(shape), dt))

                def ps1(name, shape, dt):
                    return es1.enter_context(nc.psum_tensor(uq(name), list(shape), dt))

                uT = sb1("uT", [128, 8, T], BF16); TuT = [Tile() for _ in range(NT)]
                xt = [sb1(f"xt{i}", [128, 8, 512], F32) for i in range(2)]; Txt = [Tile(), Tile()]
                xsq = sb1("xsq", [128, 8, 512], BF16); Txsq = Tile()
                rstd = sb1("rstd", [128, 512], F32); Trstd = Tile()
                tmpn = [sb1(f"tmpn{i}", [128, 512], F32) for i in range(2)]; Ttmpn = [Tile(), Tile()]
                pss = ps1("pss", [128, 512], F32); Tpss = Tile()
                psg = [ps1(f"psg{i}", [128, 512], F32) for i in range(6)]; Tpsg = [Tile() for _ in range(6)]
                wb = [sb1(f"wb{i}", [128, 8, 512], BF16) for i in range(2)]; Twb = [Tile(), Tile()]
                wdt = sb1("wdt", [128, 8, 32], BF16); Twdt = Tile()
                stgb = [sb1(f"stgb{i}", [128, T], BF16) for i in range(2)]; Tstgb = [[Tile() for _ in range(NT)] for _ in range(2)]
                stgf = [sb1(f"stgf{i}", [128, T], F32) for i in range(2)]; Tstgf = [[Tile() for _ in range(NT)] for _ in range(2)]
                stgt = [sb1(f"stgt{i}", [128, 4, 512], BF16) for i in range(2)]; Tstgt = [[Tile() for _ in range(4)] for _ in range(2)]
                acc = [sb1(f"acc{i}", [128, 512], F32) for i in range(2)]; Tacc = [Tile(), Tile()]
                ctm = [sb1(f"ctm{i}", [128, 512], F32) for i in range(2)]; Tctm = [Tile(), Tile()]

                xTv = xT.rearrange("k p t -> p k t")
                for i in range(NT):
                    s = i % 2
                    P.dma("sp", xt[s][:], xTv[:, :, i * 512:(i + 1) * 512], [td(("xT", i))], [Txt[s]])
                    ACT(P, xsq[:], xt[s][:], AF.Square, [Txt[s]], [Txsq])
                    for k in range(8):
                        MM(P, pss[:], onesb[:], xsq[:, k, :], k == 0, k == 7, [Tones, Txsq], [Tpss])
                    ACT(P, rstd[:], pss[:], AF.Ln, [Tpss], [Trstd], scale=1.0 / D, bias=EPS)
                    ACT(P, rstd[:], rstd[:], AF.Exp, [Trstd], [Trstd], scale=-0.5)
                    for k in range(8):
                        tm, Ttm = tmpn[k % 2], Ttmpn[k % 2]
                        STT(P, tm[:], xt[s][:, k, :], A1[:, l, k:k + 1], rstd[:], ALU.mult, ALU.mult, [Txt[s], Tmod, Trstd], [Ttm])
                        ACT(P, uT[:, k, i * 512:(i + 1) * 512], tm[:], AF.Identity, [Ttm, Tmod], [TuT[i]], bias=modT[:, l, k:k + 1])
                if uTd is not None and l == 0:
                    P.dma("sp", uTd.rearrange("k p t -> p k t"), uT[:], TuT, [td("uTd")])

                wsrc = w_in[l].rearrange("(k p) c -> p k c", p=128)
                groups = [("q", 0, 0), ("f", 1024, 0), ("f", 1536, 1), ("xbc", 3584, 0), ("xbc", 4096, 1), ("xbc", 4608, 2),
                          ("gate", 5152, 0), ("gate", 5664, 1), ("gate", 6176, 2), ("gate", 6688, 3),
                          ("v", 512, 0), ("ga", 2048, 0), ("z", 2560, 0), ("z", 3072, 1), ("dt", 5120, 0)]
                gi = 0
                pg = 0
                sbi = 0
                sfi = 0
                sti = 0
                for kind, c0, sub in groups:
                    if kind == "dt":
                        P.dma("pool", wdt[:], wsrc[:, :, c0:c0 + 32], (), [Twdt])
                        sf0, Tsf0 = stgf[sfi % 2], Tstgf[sfi % 2]; sfi += 1
                        sf1, Tsf1 = stgf[sfi % 2], Tstgf[sfi % 2]; sfi += 1
                        for i in range(NT):
                            ps, Tps = psg[pg % 6], Tpsg[pg % 6]; pg += 1
                            for k in range(8):
                                MM(P, ps[0:32, :], wdt[:, k, :], uT[:, k, i * 512:(i + 1) * 512], k == 0, k == 7, [Twdt, TuT[i]], [Tps])
                            sl = slice(i * 512, (i + 1) * 512)
                            ACT(P, sf0[0:32, sl], ps[0:32, :], AF.Exp, [Tps, Tpv], [Tsf0[i]], bias=pv[0:32, l * PVL + PV_DTB:l * PVL + PV_DTB + 1])
                            ACT(P, sf0[0:32, sl], sf0[0:32, sl], AF.Ln, [Tsf0[i]], [Tsf0[i]], bias=1.0)
                            TS(P, "dve", sf1[0:32, sl], sf0[0:32, sl], nA[:, l:l + 1], None, ALU.mult, None, [Tsf0[i], TnA], [Tsf1[i]])
                        P.dma("sp", dtT[:, :], sf0[0:32, :], Tsf0, [td("dtT")])
                        P.dma("sp", aT[:, :], sf1[0:32, :], Tsf1, [td("aT")])
                        continue
                    w, Tw = wb[gi % 2], Twb[gi % 2]; gi += 1
                    P.dma("pool", w[:], wsrc[:, :, c0:c0 + 512], (), [Tw])
                    if kind in ("v", "ga", "z"):
                        dst = {"v": vS, "ga": gaS, "z": zS}[kind]
                        dcol = sub * 512
                        fn = AF.Copy if kind == "v" else AF.Silu
                        for i in range(NT):
                            st, Tst = stgt[sti % 2], Tstgt[sti % 2]; sti += 1
                            for b4 in range(4):
                                b = i * 4 + b4
                                ps, Tps = psg[pg % 6], Tpsg[pg % 6]; pg += 1
                                for k in range(8):
                                    MM(P, ps[:], uT[:, k, b * 128:(b + 1) * 128], w[:, k, :], k == 0, k == 7, [Tw, TuT[i]], [Tps])
                                ACT(P, st[:, b4, :], ps[:], fn, [Tps], [Tst[b4]])
                            dv = dst[i * 512:(i + 1) * 512, dcol:dcol + 512].rearrange("(b p) c -> p b c", p=128)
                            P.dma("sp", dv, st[:], Tst, [td((kind, i))])
                        continue
                    for t4 in range(4):
                        tix = sub * 4 + t4
                        if kind == "f":
                            sg, Tsg = stgf[sfi % 2], Tstgf[sfi % 2]; sfi += 1
                        else:
                            sg, Tsg = stgb[sbi % 2], Tstgb[sbi % 2]; sbi += 1
                        for i in range(NT):
                            ps, Tps = psg[pg % 6], Tpsg[pg % 6]; pg += 1
                            for k in range(8):
                                MM(P, ps[:], w[:, k, t4 * 128:(t4 + 1) * 128], uT[:, k, i * 512:(i + 1) * 512], k == 0, k == 7, [Tw, TuT[i]], [Tps])
                            sl = slice(i * 512, (i + 1) * 512)
                            if kind == "q":
                                ACT(P, sg[:, sl], ps[:], AF.Silu, [Tps], [Tsg[i]])
                            elif kind == "f":
                                a, Ta = acc[i % 2], Tacc[i % 2]
                                ACT(P, a[:], ps[:], AF.Tanh, [Tps], [Ta], scale=0.5)
                                cc = sub * 4 + t4
                                TS(P, "dve", sg[:, sl], a[:], fA[:, l, cc:cc + 1], fB[:, l, cc:cc + 1], ALU.mult, ALU.add, [Ta, TfA], [Tsg[i]])
                            elif kind == "gate":
                                a, Ta = acc[i % 2], Tacc[i % 2]
                                ACT(P, a[:], ps[:], AF.Tanh, [Tps], [Ta], scale=0.5)
                                TS(P, "pool", sg[:, sl], a[:], 0.5, 0.5, ALU.mult, ALU.add, [Ta], [Tsg[i]])
                            else:
                                a, Ta = acc[i % 2], Tacc[i % 2]
                                conv_tile(P, a, Ta, ps, Tps, pvl(l, PV_CW + 12 + tix), pvl(l, PV_CB + tix),
                                          pvl(l, PV_CW + tix), pvl(l, PV_CW + 24 + tix),
                                          cwp[:, l, 0, tix:tix + 1], cwp[:, l, 1, tix:tix + 1], [Tpv, Tcwp], ctm[i % 2], Tctm[i % 2])
                                ACT(P, sg[:, sl], a[:], AF.Silu, [Ta], [Tsg[i]])
                        if kind == "q":
                            P.dma("sp", qT[tix * 128:(tix + 1) * 128, :], sg[:], Tsg, [td(("qT", tix))])
                        elif kind == "f":
                            P.dma("sp", fT[sub, t4 * 128:(t4 + 1) * 128, :], sg[:], Tsg, [td(("fT", sub, t4))])
                        elif kind == "gate":
                            P.dma("sp", gtT[tix * 128:(tix + 1) * 128, :], sg[:], Tsg, [td(("gtT", tix))])
                        else:
                            P.dma("sp", xbcT[tix * 128:(tix + 1) * 128, :], sg[:], Tsg, [td(("xbcT", tix))])
            P.barrier()
            if stop == "p1":
                break

            phase2(P, nc, l, dict(con=con, Tcon=Tcon, identb=identb, Tidb=Tidb, flg=flg, Tflg=Tflg, pv=pv, Tpv=Tpv,
                                  qT=qT, fT=fT, vS=vS, xbcT=xbcT, dtT=dtT, aT=aT, oyS=oyS, hst=hst, sst=sst,
                                  hinit=hinit, sinit=sinit, rows_in=rows_in, sel_in=sel_in))
            P.barrier()
            if stop == "p2":
                break

            phase3(P, nc, l, NL, dict(con=con, Tcon=Tcon, identb=identb, Tidb=Tidb, onesb=onesb, Tones=Tones, pv=pv, Tpv=Tpv,
                                      modT=modT, Tmod=Tmod, A2=A2, fwp=fwp, Tcwp=Tcwp, oyS=oyS, gaS=gaS, zS=zS, gtT=gtT,
                                      xT=xT, h2T=h2T, rows_in=rows_in, w_br_a=w_br_a, w_br_b=w_br_b, w_out=w_out,
                                      w_up=w_up, w_down=w_down, td=td, u2Td=u2Td, obTd=obTd, stop=stop))
            P.barrier()

        with contextlib.ExitStack() as es4:
            def sb4(name, shape, dt):
                return es4.enter_context(nc.sbuf_tensor(uq(name), list(shape), dt))
            lnfB = sb4("lnfB", [128, D], F32); Tlnf = Tile()
            P.dma("sp", lnfB[:], lnf_in.partition_broadcast(128), (), [Tlnf])
            xb = [sb4(f"xb{i}", [128, 8, 128], F32) for i in range(2)]; Txb = [Tile(), Tile()]
            yo = [sb4(f"yo{i}", [128, D], F32) for i in range(2)]; Tyo = [Tile(), Tile()]
            junk = sb4("junk", [128, D], F32); Tjunk = Tile()
            ssq = [sb4(f"ssq{i}", [128, 1], F32) for i in range(2)]; Tssq = [Tile(), Tile()]
            psy = [es4.enter_context(nc.psum_tensor(uq(f"psy{i}"), [128, D], F32)) for i in range(2)]; Tpsy = [Tile(), Tile()]
            xTv = xT.rearrange("k p t -> p k t")
            for b in range(NB):
                s = b % 2
                P.dma("sp", xb[s][:], xTv[:, :, b * 128:(b + 1) * 128], [td(("xT", b // 4))], [Txb[s]])
                for k in range(8):
                    TR(P, psy[s][:, k * 128:(k + 1) * 128], xb[s][:, k, :], identf, [Txb[s], Tcon], [Tpsy[s]])
                ACT(P, junk[:], psy[s][:], AF.Square, [Tpsy[s]], [Tjunk, Tssq[s]], accum=ssq[s][:])
                ACT(P, ssq[s][:], ssq[s][:], AF.Ln, [Tssq[s]], [Tssq[s]], scale=1.0 / D, bias=EPS)
                ACT(P, ssq[s][:], ssq[s][:], AF.Exp, [Tssq[s]], [Tssq[s]], scale=-0.5)
                STT(P, yo[s][:], psy[s][:], ssq[s][:, 0:1], lnfB[:], ALU.mult, ALU.mult, [Tpsy[s], Tssq[s], Tlnf], [Tyo[s]])
                P.dma("sp", y_out[b * 128:(b + 1) * 128, :], yo[s][:], [Tyo[s]], [td(("y", b))])
        stats = P.emit()
    return nc, stats


def conv_tile(P, a, Ta, ps, Tps, w1, b, w0, w2, w0p, w2p, extra, tmp=None, Ttmp=None):
    ACT(P, a[:], ps[:], AF.Identity, [Tps] + extra, [Ta], scale=w1, bias=b)
    av = a[:].rearrange("p (r c) -> p r c", c=64)
    pv_ = ps[:].rearrange("p (r c) -> p r c", c=64)
    STT(P, av[:, :, 1:64], pv_[:, :, 0:63], w0, av[:, :, 1:64], ALU.mult, ALU.add, [Tps, Ta] + extra, [Ta])
    STT(P, av[:, :, 0:63], pv_[:, :, 1:64], w2, av[:, :, 0:63], ALU.mult, ALU.add, [Tps, Ta] + extra, [Ta])
    a4 = a[:].rearrange("p (h r c) -> p h r c", h=2, r=4)
    p4 = ps[:].rearrange("p (h r c) -> p h r c", h=2, r=4)
    STT(P, a4[:, :, 1:4, 0], p4[:, :, 0:3, 63], w0p, a4[:, :, 1:4, 0], ALU.mult, ALU.add, [Tps, Ta] + extra, [Ta])
    STT(P, a4[:, :, 0:3, 63], p4[:, :, 1:4, 0], w2p, a4[:, :, 0:3, 63], ALU.mult, ALU.add, [Tps, Ta] + extra, [Ta])


_UNIQ = [0]


def uq(name):
    _UNIQ[0] += 1
    return f"{name}_u{_UNIQ[0]}"


def mk(ap, dims, off=0):
    return bass.AP(ap.tensor, ap.offset + off, [ap.ap[0]] + list(dims))


import os


def phase2(P, nc, l, G):
    P.enabled = set(os.environ.get('P2OPT', 'load,hprep,sprep,hgrn_a,hu,ho,ssd_a,ssd_a2,ssd_a3,ssd_a4,ssd_l,ssd_c,sy,ss,out').split(','))
    NSTEP = int(os.environ.get('P2STEPS', NSEG))
    con, Tcon, identb, Tidb, flg, Tflg = G["con"], G["Tcon"], G["identb"], G["Tidb"], G["flg"], G["Tflg"]
    qT, fT, vS, xbcT, dtT, aT, oyS, hst, sst = G["qT"], G["fT"], G["vS"], G["xbcT"], G["dtT"], G["aT"], G["oyS"], G["hst"], G["sst"]
    identf = con[:, C_ID:C_ID + 128]
    with contextlib.ExitStack() as es2:
        def sb(name, shape, dt):
            return es2.enter_context(nc.sbuf_tensor(uq(name), list(shape), dt))

        def ps(name, shape, dt):
            return es2.enter_context(nc.psum_tensor(uq(name), list(shape), dt))

        b0 = ps("p2b0", [128, 512], F32); Tb0 = Tile()
        b1 = ps("p2b1", [128, 512], F32); Tb1 = Tile()
        b2 = ps("p2b2", [128, 1024], F32); Tb2 = [Tile(), Tile()]
        b4 = ps("p2b4", [128, 1024], BF16); Tb4 = Tile()
        b5 = ps("p2b5", [128, 1024], F32); Tb5 = Tile()
        b7 = ps("p2b7", [128, 512], F32); Tb7 = Tile()

        sel = sb("sel", [16, NSEL], F32); Tsel = Tile()
        P.dma("sp", sel[:], G["sel_in"][:, :], (), [Tsel])
        dskB = sb("dskB", [128, 16], F32); Tdsk = Tile()
        P.dma("sp", dskB[:], G["rows_in"][l:l + 1, 1536:1552].partition_broadcast(128), (), [Tdsk])

        q_sb = [sb(f"q_sb{i}", [128, 4, 256], BF16) for i in range(2)]
        f_sb = [sb(f"f_sb{i}", [128, 4, 256], F32) for i in range(2)]
        v_sb = [sb(f"v_sb{i}", [128, 2, 512], BF16) for i in range(2)]
        x_sb = [sb(f"x_sb{i}", [128, 12, 256], BF16) for i in range(2)]
        a_sb = [sb(f"a_sb{i}", [16, 256], F32) for i in range(2)]
        stkA = [sb(f"stkA{i}", [32, 256], F32) for i in range(2)]
        stkB = [sb(f"stkB{i}", [16, 256], F32) for i in range(2)]
        ecum = [sb(f"ecum{i}", [16, 256], F32) for i in range(2)]
        Pp = [sb(f"Pp{i}", [128, 4, 256], F32) for i in range(2)]
        qt = [sb(f"qt{i}", [128, 4, 256], BF16) for i in range(2)]
        kt = [sb(f"kt{i}", [128, 4, 256], BF16) for i in range(2)]
        kh = [sb(f"kh{i}", [128, 4, 256], BF16) for i in range(2)]
        Tq, Tf, Tv, Tx, Ta, TsA, TsAd, TsB, Tec, TPp, Tqt, Tkt, Tkh = ([Tile(), Tile()] for _ in range(13))
        d0 = sb("d0", [128, 4, 256], F32); d1 = sb("d1", [128, 4, 256], F32); Td01 = Tile()
        Pinv = sb("Pinv", [128, 4, 256], F32); TPinv = Tile()
        kk = sb("kk", [128, 4, 256], F32); Tkk = Tile()
        ktmp = sb("ktmp", [128, 4, 256], F32); Tktmp = Tile()
        ATm = sb("ATm", [128, 4, 128], BF16); TATm = Tile()
        khtok = sb("khtok", [128, 4, 128], BF16); Tkhtok = Tile()
        qpad = [sb(f"qpad{i}", [128, 4, 4, 128], BF16) for i in range(2)]; Tqpad = [Tile(), Tile()]
        Sbf = [sb(f"Sbf{i}", [128, 4, 4, 128], BF16) for i in range(2)]; TSbf = [[Tile() for _ in range(4)] for _ in range(2)]
        tmpS = sb("tmpS", [128, 512], F32); TtmpS = Tile()
        oy = [sb(f"oy{i}", [128, 1536], F32) for i in range(2)]; Toy = [[Tile(), Tile()] for _ in range(2)]
        mnb = sb("mnb", [128, 2, 4, 128], BF16); Tmnb = Tile()
        ncum = sb("ncum", [128, 16], F32); Tncum = Tile()
        eal = sb("eal", [128, 8], F32); Teal = Tile()
        Btok = sb("Btok", [128, 256], BF16); TBtok = Tile()
        xstok = sb("xstok", [128, 1024], BF16); Txstok = Tile()
        tokA = sb("tokA", [128, 48], F32); TtokA = Tile()
        wt = sb("wt", [128, 16], F32); dtw = sb("dtw", [128, 16], F32); Twt = Tile()
        xdt = sb("xdt", [128, 1024], BF16); Txdt = Tile()
        xw = sb("xw", [128, 1024], BF16); Txw = Tile()
        Lf = sb("Lf", [128, 16, 128], F32); TLf = [Tile() for _ in range(4)]
        Mt = sb("Mt", [128, 16, 128], BF16); TMt = [Tile() for _ in range(4)]
        Ct = sb("Ct", [128, 2, 4, 128], BF16); TCt = Tile()
        STz = sb("STz", [128, 2, 512], BF16); TSTz = Tile()
        CTz = [sb(f"CTz{i}", [128, 2, 2, 256], BF16) for i in range(2)]; TCTz = [Tile(), Tile()]
        Btz = sb("Btz", [128, 2, 2, 128], BF16); TBtz = Tile()
        tmpST = sb("tmpST", [128, 512], F32); TtmpST = Tile()
        dsx = sb("dsx", [128, 1024], BF16); Tdsx = Tile()
        Sh = [[sb(f"Sh{d}_{i}", [128, 512], F32) for i in range(3)] for d in range(2)]; TSh = [[Tile() for _ in range(3)] for _ in range(2)]
        ST = [[sb(f"ST{d}_{i}", [128, 512], F32) for i in range(3)] for d in range(2)]; TST = [[Tile() for _ in range(3)] for _ in range(2)]
        shi = [0, 0]
        sti = [0, 0]
        for i in range(2):
            MEMSET(P, "pool", qpad[i][:], 0.0, [Tqpad[i]])
            MEMSET(P, "pool", CTz[i][:], 0.0, [TCTz[i]])
        MEMSET(P, "pool", STz[:], 0.0, [TSTz])
        for rep in range(4):
            CP(P, "dve", mnb[:, :, rep, :], con[:, C_MNF:C_MNF + 256].rearrange("p (a t) -> p a t", a=2), [Tcon], [Tmnb])
        MEMSET(P, "pool", Btz[:], 0.0, [TBtz])

        def mask3(c0_, n):
            return mk(con[:, c0_:c0_ + n], [(0, 4), (1, n)])

        nstep = 0
        nblk = 0
        for step in range(NSTEP):
            for d in range(2):
                seg = step if d == 0 else NSEG - 1 - step
                s = nstep % 2
                nstep += 1
                tsl = slice(seg * 256, (seg + 1) * 256)
                e32 = 31 if d == 0 else 0
                e128 = 127 if d == 0 else 0
                P.sec = 'load'
                P.dma("sp", q_sb[s][:], qT.rearrange("(h p) t -> p h t", p=128)[:, :, tsl], (), [Tq[s]])
                P.dma("sp", f_sb[s][:], fT[d].rearrange("(h p) t -> p h t", p=128)[:, :, tsl], (), [Tf[s]])
                P.dma("sp", v_sb[s][:], vS[tsl, :].rearrange("(b p) c -> p b c", p=128), (), [Tv[s]])
                P.dma("sp", x_sb[s][:], xbcT.rearrange("(k p) t -> p k t", p=128)[:, :, tsl], (), [Tx[s]])
                P.dma("sp", a_sb[s][:], aT[d * 16:(d + 1) * 16, tsl], (), [Ta[s]])
                P.dma("sp", stkA[s][16:32, :], dtT[d * 16:(d + 1) * 16, tsl], (), [TsAd[s]])
                if step == 0:
                    P.dma("sp", Sh[d][0][:], G["hinit"][l, d], (), [TSh[d][0]])
                    P.dma("sp", ST[d][0][:], G["sinit"][l, d], (), [TST[d][0]])
                P.sec = 'hprep'
                if d == 0:
                    fsrc = f_sb[s][:]
                    pout = Pp[s][:].rearrange("p h t -> p (h t)")
                else:
                    fsrc = mk(f_sb[s][:], [(-256, 4), (-1, 256)], 1023)
                    pout = mk(Pp[s][:], [(-1, 1024)], 1023)
                TT(P, "pool", d0[:], fsrc, mask3(C_NR32, 256), ALU.mult, [Tf[s], Tcon], [Td01])
                TT(P, "pool", d1[:], fsrc, mask3(C_R32, 256), ALU.mult, [Tf[s], Tcon], [Td01])
                SCAN(P, pout, d0[:].rearrange("p h t -> p (h t)"), d1[:].rearrange("p h t -> p (h t)"), 0.0, [Td01], [TPp[s]])
                P.op("dve", lambda h, o=Pinv[:], i=Pp[s][:]: h.reciprocal(out=o, in_=i), [TPp[s]], [TPinv], cost=6400.0)
                TT(P, "pool", qt[s][:], q_sb[s][:], Pp[s][:], ALU.mult, [Tq[s], TPp[s]], [Tqt[s]])
                TS(P, "pool", kk[:], f_sb[s][:], -1.0, 1.0, ALU.mult, ALU.add, [Tf[s]], [Tkk])
                TT(P, "dve", ktmp[:], kk[:], Pinv[:], ALU.mult, [Tkk, TPinv], [Tktmp])
                CP(P, "act", kt[s][:], ktmp[:], [Tktmp], [Tkt[s]])
                TT(P, "dve", kh[s][:].rearrange("p h (c j) -> p (h c) j", j=32), ktmp[:].rearrange("p h (c j) -> p (h c) j", j=32),
                   mk(Pp[s][:], [(32, 32), (0, 32)], e32), ALU.mult, [Tktmp, TPp[s]], [Tkh[s]])
                P.sec = 'sprep'
                if d == 0:
                    SCAN(P, stkA[s][0:16, :], con[0:16, C_NR128:C_NR128 + 256], a_sb[s][:], 0.0, [Ta[s], Tcon], [TsA[s]])
                else:
                    SCAN(P, mk(stkA[s][0:16, :], [(-1, 256)], 255), con[0:16, C_NR128:C_NR128 + 256], mk(a_sb[s][:], [(-1, 256)], 255), 0.0, [Ta[s], Tcon], [TsA[s]])
                ACT(P, ecum[s][:], stkA[s][0:16, :], AF.Exp, [TsA[s]], [Tec[s]])
                CP(P, "pool", CTz[s][0:64, :, 0, :], x_sb[s][0:64, 10:12, :], [Tx[s]], [TCTz[s]])
                CP(P, "pool", CTz[s][64:128, :, 1, :], x_sb[s][64:128, 10:12, :], [Tx[s]], [TCTz[s]])
                TT(P, "dve", stkB[s][:].rearrange("p (b t) -> p b t", b=2), mk(stkA[s][0:16, :], [(128, 2), (0, 128)], e128),
                   stkA[s][0:16, :].rearrange("p (b t) -> p b t", b=2), ALU.subtract, [TsA[s]], [TsB[s]])

                for bi in ((0, 1) if d == 0 else (1, 0)):
                    blk = seg * 2 + bi
                    c0 = bi * 128
                    o = nblk % 2
                    nblk += 1
                    P.sec = 'hgrn_a'
                    for h in range(4):
                        MM(P, b0[:, h * 128:(h + 1) * 128], kt[s][:, h, c0:c0 + 128], qt[s][:, h, c0:c0 + 128], True, True, [Tkt[s], Tqt[s]], [Tb0])
                    TT(P, "dve", ATm[:], b0[:].rearrange("p (h t) -> p h t", h=4), mask3(C_HMF if d == 0 else C_HMB, 128), ALU.mult, [Tb0, Tcon], [TATm])
                    for h in range(4):
                        TR(P, b4[:, h * 128:(h + 1) * 128], kh[s][:, h, c0:c0 + 128], identb[:], [Tkh[s], Tidb], [Tb4])
                    CP(P, "act", khtok[:].rearrange("p h k -> p (h k)"), b4[:, 0:512], [Tb4], [Tkhtok])
                    CP(P, "act", mk(qpad[o][:], [(512, 4), (160, 4), (1, 32)]), mk(qt[s][:], [(256, 4), (32, 4), (1, 32)], c0), [Tqt[s]], [Tqpad[o]])
                    P.sec = 'hu'
                    for ci, c in enumerate((0, 1, 2, 3) if d == 0 else (3, 2, 1, 0)):
                        cur = shi[d]
                        nxt = (cur + 1) % 3
                        CP(P, "act", Sbf[o][:, c].rearrange("p h v -> p (h v)"), Sh[d][cur][:], [TSh[d][cur]], [TSbf[o][c]])
                        ub = ci % 2
                        for h in range(4):
                            MM(P, b2[:, ub * 512 + h * 128:ub * 512 + (h + 1) * 128], khtok[32 * c:32 * c + 32, h, :],
                               v_sb[s][32 * c:32 * c + 32, bi, h * 128:(h + 1) * 128], True, True, [Tkhtok, Tv[s]], [Tb2[ub]], tp=(32 * c, 0))
                        TT(P, "dve", tmpS[:].rearrange("p (h v) -> p h v", h=4), Sh[d][cur][:].rearrange("p (h v) -> p h v", h=4),
                           mk(Pp[s][:], [(256, 4), (0, 128)], c0 + c * 32 + e32), ALU.mult, [TSh[d][cur], TPp[s]], [TtmpS])
                        TT(P, "dve", Sh[d][nxt][:], tmpS[:], b2[:, ub * 512:(ub + 1) * 512], ALU.add, [TtmpS, Tb2[ub]], [TSh[d][nxt]])
                        shi[d] = nxt
                    P.sec = 'ho'
                    for h in range(4):
                        MM(P, b1[:, h * 128:(h + 1) * 128], ATm[:, h, :], v_sb[s][:, bi, h * 128:(h + 1) * 128], True, False, [TATm, Tv[s]], [Tb1])
                        for c in range(4):
                            MM(P, b1[:, h * 128:(h + 1) * 128], qpad[o][:, h, c, :], Sbf[o][:, c, h, :], False, c == 3, [Tqpad[o], TSbf[o][c]], [Tb1])
                    CP(P, "act", oy[o][:, 0:512], b1[:], [Tb1], [Toy[o][0]])

                    P.sec = 'ssd_a'
                    for g in range(4):
                        MM(P, b0[:, g * 128:(g + 1) * 128], x_sb[s][:, 8 + g // 2, c0:c0 + 128], CTz[s][:, g // 2, g % 2, c0:c0 + 128],
                           True, True, [Tx[s], TCTz[s]], [Tb0])
                    P.sec = 'ssd_a2'
                    for k in range(8):
                        TR(P, b4[:, k * 128:(k + 1) * 128], x_sb[s][:, k, c0:c0 + 128], identb[:], [Tx[s], Tidb], [Tb4])
                    CP(P, "act", xstok[:], b4[:], [Tb4], [Txstok])
                    for k in range(2):
                        TR(P, b4[:, k * 128:(k + 1) * 128], x_sb[s][:, 8 + k, c0:c0 + 128], identb[:], [Tx[s], Tidb], [Tb4])
                    b4v = b4[:, 0:256].rearrange("p (g c) -> p g c", g=2)
                    CP(P, "dve", Btz[:, 0, :, 0:64], b4v[:, :, 0:64], [Tb4], [TBtz])
                    CP(P, "dve", Btz[:, 1, :, 64:128], b4v[:, :, 64:128], [Tb4], [TBtz])
                    P.sec = 'ssd_a3'
                    TR(P, b1[:, 0:32], stkA[s][0:32, c0:c0 + 128], identf[0:32, 0:32], [TsA[s], TsAd[s], Tcon], [Tb1])
                    TR(P, b1[:, 32:48], stkB[s][0:16, c0:c0 + 128], identf[0:16, 0:16], [TsB[s], Tcon], [Tb1])
                    CP(P, "dve", tokA[:], b1[:, 0:48], [Tb1], [TtokA])
                    P.sec = 'ssd_a4'
                    ACT(P, wt[:], tokA[:, 32:48], AF.Exp, [TtokA], [Twt])
                    TT(P, "dve", dtw[:], tokA[:, 16:32], wt[:], ALU.mult, [TtokA, Twt], [Twt])
                    TS(P, "dve", ncum[:], tokA[:, 0:16], -1.0, None, ALU.mult, None, [TtokA], [Tncum])
                    TT(P, "pool", xdt[:].rearrange("p (h q) -> p h q", h=16), xstok[:].rearrange("p (h q) -> p h q", h=16),
                       mk(tokA[:], [(1, 16), (0, 64)], 16), ALU.mult, [Txstok, TtokA], [Txdt])
                    TT(P, "dve", xw[:].rearrange("p (h q) -> p h q", h=16), xstok[:].rearrange("p (h q) -> p h q", h=16),
                       mk(dtw[:], [(1, 16), (0, 64)]), ALU.mult, [Txstok, Twt], [Txw])
                    P.sec = 'ssd_l'
                    for gg in range(4):
                        MM(P, b1[:], identb[:], mnb[:, d].rearrange("p r t -> p (r t)"), True, False, [Tidb, Tmnb], [Tb1])
                        for hh in range(4):
                            hd = gg * 4 + hh
                            MM(P, b1[:, hh * 128:(hh + 1) * 128], sel[0:16, S_SELH + hd * 128:S_SELH + (hd + 1) * 128], stkA[s][0:16, c0:c0 + 128],
                               False, hh == 3, [Tsel, TsA[s]], [Tb1])
                        for hh in range(4):
                            hd = gg * 4 + hh
                            ACT(P, Lf[:, hd, :], b1[:, hh * 128:(hh + 1) * 128], AF.Exp, [Tb1, Tncum], [TLf[gg]], bias=ncum[:, hd:hd + 1])
                        TT(P, "dve", Mt[:, gg * 4:(gg + 1) * 4, :], Lf[:, gg * 4:(gg + 1) * 4, :], mk(b0[:], [(0, 4), (1, 128)], gg * 128), ALU.mult,
                           [TLf[gg], Tb0], [TMt[gg]])
                    P.sec = 'ssd_c'
                    for q in range(8):
                        MM(P, b2[:, q * 128:(q + 1) * 128], sel[0:16, S_SEL2 + q * 128:S_SEL2 + (q + 1) * 128], ecum[s][0:16, c0:c0 + 128],
                           True, True, [Tsel, Tec[s]], [Tb2[q // 4]])
                    TT(P, "dve", Ct[:], mk(x_sb[s][:], [(256, 2), (0, 4), (1, 128)], 10 * 256 + c0), b2[:].rearrange("p (g h t) -> p g h t", g=2, h=4),
                       ALU.mult, [Tx[s], Tb2[0], Tb2[1]], [TCt])
                    P.sec = 'sy'
                    cur = sti[d]
                    nxt = (cur + 1) % 3
                    CP(P, "act", STz[0:64, 0, :], ST[d][cur][0:64, :], [TST[d][cur]], [TSTz])
                    CP(P, "act", STz[64:128, 1, :], ST[d][cur][64:128, :], [TST[d][cur]], [TSTz])
                    if d == 0:
                        TT(P, "pool", dsx[:].rearrange("p (h q) -> p h q", h=16), xstok[:].rearrange("p (h q) -> p h q", h=16),
                           mk(dskB[:], [(1, 16), (0, 64)]), ALU.mult, [Txstok, Tdsk], [Tdsx])
                    for h in range(16):
                        g = h // 4
                        gq, half, hh = g // 2, g % 2, h % 4
                        q = gq * 4 + hh
                        MM(P, b5[:, h * 64:(h + 1) * 64], Mt[:, h, :], xdt[:, h * 64:(h + 1) * 64], True, False, [TMt[g], Txdt], [Tb5])
                        MM(P, b5[:, h * 64:(h + 1) * 64], Ct[:, gq, hh, :], STz[:, half, q * 64:(q + 1) * 64],
                           False, d == 1, [TCt, TSTz], [Tb5])
                        if d == 0:
                            MM(P, b5[:, h * 64:(h + 1) * 64], identb[:], dsx[:, h * 64:(h + 1) * 64], False, True, [Tidb, Tdsx], [Tb5])
                    CP(P, "act", oy[o][:, 512:1536], b5[:], [Tb5], [Toy[o][1]])
                    P.sec = 'ss'
                    for g in range(4):
                        gq, half = g // 2, g % 2
                        MM(P, b7[:, gq * 256:(gq + 1) * 256], Btz[:, half, gq, :], xw[:, g * 256:(g + 1) * 256],
                           half == 0, half == 1, [TBtz, Txw], [Tb7])
                    CP(P, "dve", eal[:], mk(b2[:], [(128, 8)], e128), [Tb2[0], Tb2[1]], [Teal])
                    TT(P, "pool", tmpST[:].rearrange("p (q n) -> p q n", q=8), ST[d][cur][:].rearrange("p (q n) -> p q n", q=8),
                       mk(eal[:], [(1, 8), (0, 64)]), ALU.mult, [TST[d][cur], Teal], [TtmpST])
                    TT(P, "dve", ST[d][nxt][:], tmpST[:], b7[:], ALU.add, [TtmpST, Tb7], [TST[d][nxt]])
                    sti[d] = nxt
                    P.sec = 'out'
                    P.dma("sp", oyS[d][blk * 128:(blk + 1) * 128, :], oy[o][:], Toy[o], ())
                P.sec = 'out'
                cur = shi[d]
                P.dma("sp", hst[seg, l, d], Sh[d][cur][:], [TSh[d][cur]], ())
                cs = sti[d]
                P.dma("sp", sst[seg, l, d], ST[d][cs][:], [TST[d][cs]], ())
                if step < NSEG - 1:
                    nxt = (cur + 1) % 3
                    TS(P, "dve", Sh[d][nxt][:], Sh[d][cur][:], flg[:, 0:1], None, ALU.mult, None, [TSh[d][cur], Tflg], [TSh[d][nxt]])
                    shi[d] = nxt
                    nxt = (cs + 1) % 3
                    TS(P, "dve", ST[d][nxt][:], ST[d][cs][:], flg[:, 0:1], None, ALU.mult, None, [TST[d][cs], Tflg], [TST[d][nxt]])
                    sti[d] = nxt

        P.sec = None


def phase3(P, nc, l, NL, G):
    con, Tcon, identb, Tidb, onesb, Tones, pv, Tpv = G["con"], G["Tcon"], G["identb"], G["Tidb"], G["onesb"], G["Tones"], G["pv"], G["Tpv"]
    modT, Tmod, A2, fwp, Tcwp = G["modT"], G["Tmod"], G["A2"], G["fwp"], G["Tcwp"]
    oyS, gaS, zS, gtT, xT, h2T = G["oyS"], G["gaS"], G["zS"], G["gtT"], G["xT"], G["h2T"]
    u2Td = G["u2Td"]
    xTv = xT.rearrange("k p t -> p k t")
    u2v = u2Td.rearrange("k p t -> p k t")

    def pvl(off, n=1):
        return pv[:, l * PVL + off:l * PVL + off + n]

    obTd = G["obTd"]
    obv = obTd.rearrange("(k p) t -> p k t", p=128)
    with contextlib.ExitStack() as es3:
        def sb(name, shape, dt):
            return es3.enter_context(nc.sbuf_tensor(uq(name), list(shape), dt))

        def ps(name, shape, dt):
            return es3.enter_context(nc.psum_tensor(uq(name), list(shape), dt))

        psTr = [ps(f"psTr{i}", [128, 2048], BF16) for i in range(2)]; TpsTr = [Tile(), Tile()]
        rowsB = sb("rowsB", [128, NROW], F32); Trows = Tile()
        P.dma("sp", rowsB[:], G["rows_in"][l:l + 1, :].partition_broadcast(128), (), [Trows])
        NBUF = 3
        oyf = [sb(f"oyf{i}", [128, 1536], F32) for i in range(NBUF)]
        oyb = [sb(f"oyb{i}", [128, 1536], F32) for i in range(NBUF)]
        gab = [sb(f"gab{i}", [128, 512], BF16) for i in range(NBUF)]
        zb = [sb(f"zb{i}", [128, 1024], BF16) for i in range(NBUF)]
        Toyf, Toyb, Tgab, Tzb = ([Tile() for _ in range(NBUF)] for _ in range(4))
        ssum = [sb(f"ssum{i}", [128, 1536], F32) for i in range(2)]; Tssum = [Tile(), Tile()]
        yz = [sb(f"yz{i}", [128, 1024], F32) for i in range(2)]; Tyz = [Tile(), Tile()]
        junk = sb("junk3", [128, 256], F32); Tjunk = Tile()
        ssq = [sb(f"ssq3_{i}", [128, 8], F32) for i in range(2)]; Tssq = [Tile(), Tile()]
        t1 = [sb(f"t1_{i}", [128, 1536], F32) for i in range(2)]; Tt1 = [Tile(), Tile()]
        t2 = [sb(f"t2_{i}", [128, 512], F32) for i in range(2)]; Tt2 = [Tile(), Tile()]
        on = [sb(f"on{i}", [128, 1536], BF16) for i in range(2)]; Ton = [Tile(), Tile()]
        obs = [sb(f"obs{i}", [128, 12, 128], BF16) for i in range(2)]; Tobs = [Tile(), Tile()]
        for b in range(NB):
            s = b % NBUF
            u = b % 2
            rsl = slice(b * 128, (b + 1) * 128)
            P.dma("sp", oyf[s][:], oyS[0][rsl, :], (), [Toyf[s]])
            P.dma("sp", oyb[s][:], oyS[1][rsl, :], (), [Toyb[s]])
            P.dma("sp", gab[s][:], gaS[rsl, :], (), [Tgab[s]])
            P.dma("sp", zb[s][:], zS[rsl, :], (), [Tzb[s]])
            TT(P, "dve", ssum[u][:], oyf[s][:], oyb[s][:], ALU.add, [Toyf[s], Toyb[s]], [Tssum[u]])
            for h in range(4):
                ACT(P, junk[:, 0:128], ssum[u][:, h * 128:(h + 1) * 128], AF.Square, [Tssum[u]], [Tjunk, Tssq[u]], accum=ssq[u][:, h:h + 1])
            TT(P, "pool", yz[u][:], ssum[u][:, 512:1536], zb[s][:], ALU.mult, [Tssum[u], Tzb[s]], [Tyz[u]])
            for g in range(4):
                ACT(P, junk[:], yz[u][:, g * 256:(g + 1) * 256], AF.Square, [Tyz[u]], [Tjunk, Tssq[u]], accum=ssq[u][:, 4 + g:5 + g])
            ACT(P, ssq[u][:, 0:4], ssq[u][:, 0:4], AF.Ln, [Tssq[u]], [Tssq[u]], scale=1.0 / 128, bias=EPS)
            ACT(P, ssq[u][:, 4:8], ssq[u][:, 4:8], AF.Ln, [Tssq[u]], [Tssq[u]], scale=1.0 / 256, bias=EPS)
            ACT(P, ssq[u][:], ssq[u][:], AF.Exp, [Tssq[u]], [Tssq[u]], scale=-0.5)
            TT(P, "dve", t1[u][:, 0:512].rearrange("p (h v) -> p h v", h=4), ssum[u][:, 0:512].rearrange("p (h v) -> p h v", h=4),
               mk(ssq[u][:], [(1, 4), (0, 128)]), ALU.mult, [Tssum[u], Tssq[u]], [Tt1[u]])
            TT(P, "pool", t2[u][:], t1[u][:, 0:512], rowsB[:, 0:512], ALU.mult, [Tt1[u], Trows], [Tt2[u]])
            TT(P, "dve", on[u][:, 0:512], t2[u][:], gab[s][:], ALU.mult, [Tt2[u], Tgab[s]], [Ton[u]])
            TT(P, "dve", t1[u][:, 512:1536].rearrange("p (g v) -> p g v", g=4), yz[u][:].rearrange("p (g v) -> p g v", g=4),
               mk(ssq[u][:], [(1, 4), (0, 256)], 4), ALU.mult, [Tyz[u], Tssq[u]], [Tt1[u]])
            TT(P, "pool", on[u][:, 512:1536], t1[u][:, 512:1536], rowsB[:, 512:1536], ALU.mult, [Tt1[u], Trows], [Ton[u]])
            for k in range(12):
                TR(P, psTr[u][:, k * 128:(k + 1) * 128], on[u][:, k * 128:(k + 1) * 128], identb[:], [Ton[u], Tidb], [TpsTr[u]])
            CP(P, "act", obs[u][:], psTr[u][:, 0:1536].rearrange("p (k t) -> p k t", k=12), [TpsTr[u]], [Tobs[u]])
            P.dma("sp", obv[:, :, b * 128:(b + 1) * 128], obs[u][:], [Tobs[u]], ())
    P.barrier()

    with contextlib.ExitStack() as es3:
        def sb(name, shape, dt):
            return es3.enter_context(nc.sbuf_tensor(uq(name), list(shape), dt))

        def ps(name, shape, dt):
            return es3.enter_context(nc.psum_tensor(uq(name), list(shape), dt))

        psS = ps("p3s", [128, 512], F32); TpsS = Tile()
        psg = [ps(f"p3g{i}", [128, 512], F32) for i in range(6)]; Tpsg = [Tile() for _ in range(6)]
        wA = sb("wA", [128, 4, D], BF16); wB = sb("wB", [128, 8, D], BF16); wO = sb("wO", [128, 8, D], BF16)
        TwA, TwB, TwO = Tile(), Tile(), Tile()
        P.dma("pool", wA[:], G["w_br_a"][l].rearrange("(k p) c -> p k c", p=128), (), [TwA])
        P.dma("pool", wB[:], G["w_br_b"][l].rearrange("(k p) c -> p k c", p=128), (), [TwB])
        P.dma("pool", wO[:], G["w_out"][l].rearrange("(k p) c -> p k c", p=128), (), [TwO])
        obT = [sb(f"obT{i}", [128, 12, 512], BF16) for i in range(2)]; TobT = [Tile(), Tile()]
        gt = [sb(f"gt{i}", [128, 16, 512], BF16) for i in range(2)]; Tgt = [Tile(), Tile()]
        xt = [sb(f"xt3_{i}", [128, 8, 512], F32) for i in range(2)]; Txt = [Tile(), Tile()]
        m1 = [sb(f"m1_{i}", [128, 512], F32) for i in range(2)]; m2 = [sb(f"m2_{i}", [128, 512], F32) for i in range(2)]
        Tm1, Tm2 = [Tile(), Tile()], [Tile(), Tile()]
        mg = [sb(f"mg{i}", [128, 8, 512], BF16) for i in range(2)]; Tmg = [[Tile() for _ in range(8)] for _ in range(2)]
        xsq = sb("xsq3", [128, 8, 512], BF16); Txsq = Tile()
        rstd = sb("rstd3", [128, 512], F32); Trstd = Tile()
        tmpn = [sb(f"tmpn3_{i}", [128, 512], F32) for i in range(2)]; Ttmpn = [Tile(), Tile()]
        u2s = [sb(f"u2s{i}", [128, 8, 512], BF16) for i in range(2)]; Tu2s = [Tile(), Tile()]
        pg = 0
        for i in range(NT):
            s = i % 2
            tsl = slice(i * 512, (i + 1) * 512)
            P.dma("sp", obT[s][:], obv[:, :, tsl], (), [TobT[s]])
            P.dma("sp", gt[s][:], gtT.rearrange("(k p) t -> p k t", p=128)[:, :, tsl], (), [Tgt[s]])
            P.dma("sp", xt[s][:], xTv[:, :, tsl], (), [Txt[s]])
            for c in range(8):
                pa, Tpa = psg[pg % 6], Tpsg[pg % 6]; pg += 1
                pb, Tpb = psg[pg % 6], Tpsg[pg % 6]; pg += 1
                for k in range(4):
                    MM(P, pa[:], wA[:, k, c * 128:(c + 1) * 128], obT[s][:, k, :], k == 0, k == 3, [TwA, TobT[s]], [Tpa])
                for k in range(8):
                    MM(P, pb[:], wB[:, k, c * 128:(c + 1) * 128], obT[s][:, 4 + k, :], k == 0, k == 7, [TwB, TobT[s]], [Tpb])
                TT(P, "dve", m1[c % 2][:], pa[:], gt[s][:, c, :], ALU.mult, [Tpa, Tgt[s]], [Tm1[c % 2]])
                TT(P, "dve", m2[c % 2][:], pb[:], gt[s][:, 8 + c, :], ALU.mult, [Tpb, Tgt[s]], [Tm2[c % 2]])
                TT(P, "pool", mg[s][:, c, :], m1[c % 2][:], m2[c % 2][:], ALU.add, [Tm1[c % 2], Tm2[c % 2]], [Tmg[s][c]])
            for c in range(8):
                po, Tpo = psg[pg % 6], Tpsg[pg % 6]; pg += 1
                for k in range(8):
                    MM(P, po[:], wO[:, k, c * 128:(c + 1) * 128], mg[s][:, k, :], k == 0, k == 7, [TwO, Tmg[s][k]], [Tpo])
                STT(P, xt[s][:, c, :], po[:], modT[:, l, 16 + c:17 + c], xt[s][:, c, :], ALU.mult, ALU.add, [Tpo, Tmod, Txt[s]], [Txt[s]])
            P.dma("sp", xTv[:, :, tsl], xt[s][:], [Txt[s]], ())
            ACT(P, xsq[:], xt[s][:], AF.Square, [Txt[s]], [Txsq])
            for k in range(8):
                MM(P, psS[:], onesb[:], xsq[:, k, :], k == 0, k == 7, [Tones, Txsq], [TpsS])
            ACT(P, rstd[:], psS[:], AF.Ln, [TpsS], [Trstd], scale=1.0 / D, bias=EPS)
            ACT(P, rstd[:], rstd[:], AF.Exp, [Trstd], [Trstd], scale=-0.5)
            for k in range(8):
                tm, Ttm = tmpn[k % 2], Ttmpn[k % 2]
                STT(P, tm[:], xt[s][:, k, :], A2[:, l, k:k + 1], rstd[:], ALU.mult, ALU.mult, [Txt[s], Tmod, Trstd], [Ttm])
                ACT(P, u2s[s][:, k, :], tm[:], AF.Identity, [Ttm, Tmod], [Tu2s[s]], bias=modT[:, l, 24 + k:25 + k])
            P.dma("sp", u2v[:, :, tsl], u2s[s][:], [Tu2s[s]], ())
    P.barrier()
    if G["stop"] == "p3ab":
        return

    with contextlib.ExitStack() as es3:
        def sb(name, shape, dt):
            return es3.enter_context(nc.sbuf_tensor(uq(name), list(shape), dt))

        def ps(name, shape, dt):
            return es3.enter_context(nc.psum_tensor(uq(name), list(shape), dt))

        psg = [ps(f"p3c{i}", [128, 512], F32) for i in range(6)]; Tpsg = [Tile() for _ in range(6)]
        u2T = sb("u2T", [128, 8, T], BF16); Tu2 = [Tile() for _ in range(NT)]
        for i in range(NT):
            P.dma("sp", u2T[:, :, i * 512:(i + 1) * 512], u2v[:, :, i * 512:(i + 1) * 512], (), [Tu2[i]])
        wu = [sb(f"wu{i}", [128, 8, 2, 128], BF16) for i in range(2)]; Twu = [Tile(), Tile()]
        acA = [sb(f"acA{i}", [128, 512], F32) for i in range(2)]; TacA = [Tile(), Tile()]
        acB = [sb(f"acB{i}", [128, 512], F32) for i in range(2)]; TacB = [Tile(), Tile()]
        sa = [sb(f"sa{i}", [128, 512], F32) for i in range(2)]; Tsa = [Tile(), Tile()]
        ctA = [sb(f"ctA{i}", [128, 512], F32) for i in range(2)]; TctA = [Tile(), Tile()]
        ctB = [sb(f"ctB{i}", [128, 512], F32) for i in range(2)]; TctB = [Tile(), Tile()]
        stg = [sb(f"stg3_{i}", [128, T], BF16) for i in range(2)]; Tstg = [[Tile() for _ in range(NT)] for _ in range(2)]
        wsrc = G["w_up"][l].rearrange("(k p) c -> p k c", p=128)
        pg = 0
        it = 0
        for j in range(22):
            w, Tw = wu[j % 2], Twu[j % 2]
            P.dma("pool", w[:, :, 0, :], wsrc[:, :, j * 128:(j + 1) * 128], (), [Tw])
            P.dma("pool", w[:, :, 1, :], wsrc[:, :, DFF + j * 128:DFF + (j + 1) * 128], (), [Tw])
            sg, Tsg = stg[j % 2], Tstg[j % 2]
            for i in range(NT):
                pa, Tpa = psg[pg % 6], Tpsg[pg % 6]; pg += 1
                pb, Tpb = psg[pg % 6], Tpsg[pg % 6]; pg += 1
                for k in range(8):
                    MM(P, pa[:], w[:, k, 0, :], u2T[:, k, i * 512:(i + 1) * 512], k == 0, k == 7, [Tw, Tu2[i]], [Tpa])
                for k in range(8):
                    MM(P, pb[:], w[:, k, 1, :], u2T[:, k, i * 512:(i + 1) * 512], k == 0, k == 7, [Tw, Tu2[i]], [Tpb])
                s = it % 2
                it += 1
                for (a, Ta, pp, Tpp, tix, ct_, Tct_) in ((acA[s], TacA[s], pa, Tpa, j, ctA[s], TctA[s]), (acB[s], TacB[s], pb, Tpb, 22 + j, ctB[s], TctB[s])):
                    conv_tile(P, a, Ta, pp, Tpp, pvl(PV_FW + 44 + tix), pvl(PV_FB + tix), pvl(PV_FW + tix), pvl(PV_FW + 88 + tix),
                              fwp[:, l, 0, tix:tix + 1], fwp[:, l, 1, tix:tix + 1], [Tpv, Tcwp], ct_, Tct_)
                ACT(P, sa[s][:], acA[s][:], AF.Silu, [TacA[s]], [Tsa[s]])
                TT(P, "pool", sg[:, i * 512:(i + 1) * 512], sa[s][:], acB[s][:], ALU.mult, [Tsa[s], TacB[s]], [Tsg[i]])
            P.dma("sp", h2T[j * 128:(j + 1) * 128, :], sg[:], Tsg, ())
    P.barrier()
    if G["stop"] == "p3c":
        return

    with contextlib.ExitStack() as es3:
        def sb(name, shape, dt):
            return es3.enter_context(nc.sbuf_tensor(uq(name), list(shape), dt))

        def ps(name, shape, dt):
            return es3.enter_context(nc.psum_tensor(uq(name), list(shape), dt))

        psg = [ps(f"p3d{i}", [128, 512], F32) for i in range(6)]; Tpsg = [Tile() for _ in range(6)]
        wd = sb("wd", [128, 22, D], BF16); Twd = [Tile(), Tile()]
        wdsrc = G["w_down"][l].rearrange("(k p) c -> p k c", p=128)
        P.dma("pool", wd[:, 0:11, :], wdsrc[:, 0:11, :], (), [Twd[0]])
        P.dma("pool", wd[:, 11:22, :], wdsrc[:, 11:22, :], (), [Twd[1]])
        h2 = [sb(f"h2_{i}", [128, 22, 512], BF16) for i in range(2)]; Th2 = [Tile(), Tile()]
        xt = [sb(f"xt4_{i}", [128, 8, 512], F32) for i in range(2)]; Txt = [Tile(), Tile()]
        pg = 0
        for i in range(NT):
            s = i % 2
            P.dma("sp", h2[s][:], h2T.rearrange("(k p) t -> p k t", p=128)[:, :, i * 512:(i + 1) * 512], (), [Th2[s]])
            P.dma("sp", xt[s][:], xTv[:, :, i * 512:(i + 1) * 512], (), [Txt[s]])
            for c in range(8):
                po, Tpo = psg[pg % 6], Tpsg[pg % 6]; pg += 1
                for k in range(22):
                    MM(P, po[:], wd[:, k, c * 128:(c + 1) * 128], h2[s][:, k, :], k == 0, k == 21, [Twd[k // 11], Th2[s]], [Tpo])
                STT(P, xt[s][:, c, :], po[:], modT[:, l, 40 + c:41 + c], xt[s][:, c, :], ALU.mult, ALU.add, [Tpo, Tmod, Txt[s]], [Txt[s]])
            P.dma("sp", xTv[:, :, i * 512:(i + 1) * 512], xt[s][:], [Txt[s]], ())


_CACHE = {}


def _consts():
    c = np.zeros((128, NCON), np.float32)
    s = np.arange(128)[:, None]
    t = np.arange(128)[None, :]
    c[:, C_ID:C_ID + 128] = (s == t)
    c[:, C_TRIF:C_TRIF + 128] = (s <= t)
    c[:, C_TRIB:C_TRIB + 128] = (s >= t)
    same = (s // 32) == (t // 32)
    c[:, C_HMF:C_HMF + 128] = same & (s <= t)
    c[:, C_HMB:C_HMB + 128] = same & (s >= t)
    tt = np.arange(512)
    c[:, C_R32:C_R32 + 512] = (tt % 32 == 0)[None, :]
    c[:, C_NR32:C_NR32 + 512] = (tt % 32 != 0)[None, :]
    c[:, C_NR128:C_NR128 + 256] = (np.arange(256) % 128 != 0)[None, :]
    c[:, C_MNF:C_MNF + 128] = np.where(s <= t, 0.0, NEGM)
    c[:, C_MNB:C_MNB + 128] = np.where(s >= t, 0.0, NEGM)
    sel = np.zeros((16, NSEL), np.float32)
    for h in range(16):
        sel[h, S_SELH + h * 128:S_SELH + (h + 1) * 128] = 1.0
    for gq in range(2):
        for hh in range(4):
            q = gq * 4 + hh
            for half in range(2):
                k = (2 * gq + half) * 4 + hh
                sel[k, S_SEL2 + q * 128 + half * 64:S_SEL2 + q * 128 + (half + 1) * 64] = 1.0
    return c, sel


def _pv_rows(inp):
    pv = np.zeros((128, NPV), np.float32)
    f = lambda a: np.asarray(a, np.float32)
    for l in range(L):
        b = l * PVL
        pv[:, b + PV_BADA:b + PV_BADA + 48] = f(inp["b_ada"][l]).reshape(48, 128).T
        pv[:, b + PV_LN1:b + PV_LN1 + 8] = f(inp["ln1"][l]).reshape(8, 128).T
        pv[:, b + PV_LN2:b + PV_LN2 + 8] = f(inp["ln2"][l]).reshape(8, 128).T
        for k in range(3):
            pv[:, b + PV_CW + k * 12:b + PV_CW + (k + 1) * 12] = f(inp["conv_w"][l, k]).reshape(12, 128).T
            pv[:, b + PV_FW + k * 44:b + PV_FW + (k + 1) * 44] = f(inp["ff_conv_w"][l, k]).reshape(44, 128).T
        pv[:, b + PV_CB:b + PV_CB + 12] = f(inp["conv_b"][l]).reshape(12, 128).T
        pv[:, b + PV_FB:b + PV_FB + 44] = f(inp["ff_conv_b"][l]).reshape(44, 128).T
        pv[0:32, b + PV_DTB] = f(inp["dt_bias"][l]).reshape(32)
        pv[0:32, b + PV_ALOG] = f(inp["a_log"][l]).reshape(32)
        for d in range(2):
            pv[:, PV_LB + l * 8 + d * 4:PV_LB + l * 8 + d * 4 + 4] = f(inp["lb_logits"][l, d]).reshape(4, 128).T
    pv[:, PV_LNF:PV_LNF + 8] = f(inp["ln_f"]).reshape(8, 128).T
    rows = np.zeros((L, NROW), np.float32)
    rows[:, 0:512] = f(inp["a_norm"])
    rows[:, 512:1536] = f(inp["b_norm"])
    rows[:, 1536:1552] = f(inp["d_skip"])
    return pv, rows


def make_in_maps(inp):
    f = lambda a: np.ascontiguousarray(np.asarray(a, np.float32))
    con, sel = _consts()
    pv, rows = _pv_rows(inp)
    shared = dict(w_ada=f(inp["w_ada"]), w_in=f(inp["w_in"]), w_br_a=f(inp["w_br_a"]), w_br_b=f(inp["w_br_b"]),
                  w_out=f(inp["w_out"]), w_ff_up=f(inp["w_ff_up"]), w_ff_down=f(inp["w_ff_down"]),
                  pv=pv, rows=rows, lnf=f(inp["ln_f"]).reshape(1, D), consts=con, sel=sel)
    xs = f(inp["x_sample"]); xp = f(inp["x_prompt"])
    sh = f(inp["state_hgrn"]); ss = f(inp["state_ssd"])
    maps = []
    for c in range(8):
        m = dict(shared)
        flags = np.zeros((128, 2), np.float32)
        if c < 4:
            m["x"] = xs[c]
            cv = f(inp["c"])[c]
            flags[:, 0] = 1.0
            m["hinit"] = np.ascontiguousarray(sh[c].transpose(0, 1, 3, 2, 4).reshape(L, 2, 128, 512))
            m["sinit"] = np.ascontiguousarray(ss[c].reshape(L, 2, 2, 2, 4, 64, 64).transpose(0, 1, 3, 6, 2, 4, 5).reshape(L, 2, 128, 512))
        else:
            x = np.zeros((T, D), np.float32)
            x[:2048] = xp[(c - 4) * 8:(c - 3) * 8].reshape(2048, D)
            m["x"] = x
            cv = f(inp["c_ctx"])
            flags[:, 1] = 1.0
            m["hinit"] = np.zeros((L, 2, 128, 512), np.float32)
            m["sinit"] = np.zeros((L, 2, 128, 512), np.float32)
        m["cv"] = np.ascontiguousarray(cv.reshape(8, 128).T)
        m["flags"] = flags
        maps.append(m)
    return maps


_NL = [L]


def kernel(**inp):
    if "nc" not in _CACHE:
        _CACHE["nc"], _CACHE["stats"] = build(NL=_NL[0])
    nc = _CACHE["nc"]
    maps = make_in_maps(inp)
    res = run_bass_kernel_spmd(nc, maps, core_ids=list(range(8)))
    R = res.results
    y_sample = np.stack([np.asarray(R[c]["y"], np.float32) for c in range(4)], axis=0)
    y_prompt = np.concatenate([np.asarray(R[c]["y"], np.float32)[:2048].reshape(8, 256, D) for c in range(4, 8)], axis=0)
    hs = np.concatenate([np.asarray(R[c]["hst"], np.float32)[:8] for c in range(4, 8)], axis=0)
    new_h = np.ascontiguousarray(hs.reshape(32, L, 2, 128, 4, 128).transpose(0, 1, 2, 4, 3, 5))
    st = np.concatenate([np.asarray(R[c]["sst"], np.float32)[:8] for c in range(4, 8)], axis=0)
    new_s = np.ascontiguousarray(st.reshape(32, L, 2, 2, 64, 2, 4, 64).transpose(0, 1, 2, 5, 3, 6, 7, 4).reshape(32, L, 2, 16, 64, 64))
    return (y_prompt, y_sample, new_h, new_s)
```
